# Optimizing a Trainium2 kernel written in Bass

```python
import jax
import jax.numpy as jnp
from jax import lax
import numpy as np

D_MODEL = 1024
BATCH = 32
SEQ = 256
DEPTH = 1
DEC_BATCH = 8
DEC_SEQ = 2048
PAST_LEN = 512

GRID_W = 64
HEAD_DIM = 64
N_Q_HEADS = 8
N_KV_HEADS = 2
GQA_GROUP = N_Q_HEADS // N_KV_HEADS
ATTN_WIDTH = N_Q_HEADS * HEAD_DIM
KV_WIDTH = N_KV_HEADS * HEAD_DIM
WINDOW = 128
BLOCK = 128
ROPE_BASE = 10000.0
ROPE_PAIRS = HEAD_DIM // 4
RWKV_HEADS = 8
RWKV_HEAD_SIZE = 64
RWKV_WIDTH = RWKV_HEADS * RWKV_HEAD_SIZE
N_DIR = 2
DECAY_RANK = 64
ICLR_RANK = 64
GATE_RANK = 128
RW_SHIFT_COLS = 3 * RWKV_WIDTH + N_DIR * DECAY_RANK + N_DIR * ICLR_RANK + GATE_RANK
W_IN_COLS = ATTN_WIDTH + 2 * KV_WIDTH + 2 * D_MODEL + RW_SHIFT_COLS
IN_SPLIT_IDX = (ATTN_WIDTH, ATTN_WIDTH + KV_WIDTH, ATTN_WIDTH + 2 * KV_WIDTH,
                ATTN_WIDTH + 2 * KV_WIDTH + D_MODEL, ATTN_WIDTH + 2 * KV_WIDTH + 2 * D_MODEL)
RW_SPLIT_IDX = (RWKV_WIDTH, 2 * RWKV_WIDTH, 3 * RWKV_WIDTH,
                3 * RWKV_WIDTH + N_DIR * DECAY_RANK,
                3 * RWKV_WIDTH + N_DIR * (DECAY_RANK + ICLR_RANK))
D_FF = 2816
CONV_W = 3
NORM_EPS = 1e-6
GN_EPS = 64e-5
MASK_VALUE = -1e30

kernel_name = 'hybrid_dit_window_gqa_rwkv7_step'


def rms_norm(x, g):
    xf = x.astype(jnp.float32)
    return xf * lax.rsqrt(jnp.mean(xf * xf, axis=-1, keepdims=True) + NORM_EPS) * g.astype(jnp.float32)


def modulation(cond, w_ada, b_ada):
    m = jax.nn.silu(cond) @ w_ada + b_ada
    return jnp.split(m[:, None, :], 6, axis=-1)


def neighbours(h):
    hp = jnp.pad(h, ((0, 0), (1, 1), (0, 0)))
    return hp[:, :-2], hp[:, 2:]


def axial_rope(x):
    T = x.shape[1]
    rows = T // GRID_W
    row = jnp.repeat(jnp.arange(rows), GRID_W).astype(jnp.float32)
    col = jnp.tile(jnp.arange(GRID_W), rows).astype(jnp.float32)
    freqs = ROPE_BASE ** (-jnp.arange(ROPE_PAIRS, dtype=jnp.float32) / ROPE_PAIRS)

    def rotate(xh, pos):
        ang = (pos[:, None] * freqs)[:, None, :]
        cos, sin = jnp.cos(ang), jnp.sin(ang)
        x1, x2 = jnp.split(xh.astype(jnp.float32), 2, axis=-1)
        return jnp.concatenate([x1 * cos - x2 * sin, x2 * cos + x1 * sin], axis=-1)

    x_row, x_col = jnp.split(x, 2, axis=-1)
    return jnp.concatenate([rotate(x_row, row), rotate(x_col, col)], axis=-1).astype(x.dtype)


def dense_context_attention(q, k, v, sink):
    B, T = q.shape[:2]
    qg = q.reshape(B, T, N_KV_HEADS, GQA_GROUP, HEAD_DIM)
    s = jnp.einsum('btkgd,bskd->bkgts', qg, k).astype(jnp.float32) * (HEAD_DIM ** -0.5)
    sk = jnp.broadcast_to(sink.reshape(N_KV_HEADS, GQA_GROUP, 1, 1).astype(jnp.float32),
                          (B, N_KV_HEADS, GQA_GROUP, T, 1))
    p = jax.nn.softmax(jnp.concatenate([s, sk], axis=-1), axis=-1)[..., :-1]
    o = jnp.einsum('bkgts,bskd->btkgd', p, v)
    return o.reshape(B, T, ATTN_WIDTH)


def banded_latent_attention(q, k, v, k_ctx, v_ctx, sink):
    B, T = q.shape[:2]
    NB = T // BLOCK
    P = k_ctx.shape[1]
    qb = q.reshape(B, NB, BLOCK, N_KV_HEADS, GQA_GROUP, HEAD_DIM)
    pad = ((0, 0), (BLOCK, BLOCK), (0, 0), (0, 0))
    kp = jnp.pad(k, pad).reshape(B, NB + 2, BLOCK, N_KV_HEADS, HEAD_DIM)
    vp = jnp.pad(v, pad).reshape(B, NB + 2, BLOCK, N_KV_HEADS, HEAD_DIM)
    kw = jnp.concatenate([kp[:, :NB], kp[:, 1:NB + 1], kp[:, 2:]], axis=2)
    vw = jnp.concatenate([vp[:, :NB], vp[:, 1:NB + 1], vp[:, 2:]], axis=2)
    blk = jnp.arange(NB)[:, None, None]
    qi = blk * BLOCK + jnp.arange(BLOCK)[None, :, None]
    kj = (blk - 1) * BLOCK + jnp.arange(3 * BLOCK)[None, None, :]
    valid = (jnp.abs(kj - qi) <= WINDOW) & (kj >= 0) & (kj < T)
    scale = HEAD_DIM ** -0.5
    s_lat = jnp.einsum('bnqkgd,bnskd->bnkgqs', qb, kw).astype(jnp.float32) * scale
    s_lat = jnp.where(valid[None, :, None, None], s_lat, MASK_VALUE)
    s_ctx = jnp.einsum('bnqkgd,bskd->bnkgqs', qb, k_ctx).astype(jnp.float32) * scale
    sk = jnp.broadcast_to(sink.reshape(1, 1, N_KV_HEADS, GQA_GROUP, 1, 1).astype(jnp.float32),
                          (B, NB, N_KV_HEADS, GQA_GROUP, BLOCK, 1))
    p = jax.nn.softmax(jnp.concatenate([s_lat, s_ctx, sk], axis=-1), axis=-1)
    p_lat = p[..., :3 * BLOCK]
    p_ctx = p[..., 3 * BLOCK:3 * BLOCK + P]
    o = (jnp.einsum('bnkgqs,bnskd->bnqkgd', p_lat, vw)
         + jnp.einsum('bnkgqs,bskd->bnqkgd', p_ctx, v_ctx))
    return o.reshape(B, T, ATTN_WIDTH)


def rwkv7_bidirectional(zr, state0, p):
    B, T = zr.shape[:2]
    f32 = jnp.float32
    mu = p['rwkv_mu']
    prev, nxt = neighbours(zr)
    zr = zr + mu[0] * (prev - zr) + mu[1] * (nxt - zr)
    r, k, v, zw, za, zg = jnp.split(zr, RW_SPLIT_IDX, axis=-1)
    zw = zw.reshape(B, T, N_DIR, DECAY_RANK)
    za = za.reshape(B, T, N_DIR, ICLR_RANK)
    logit = (p['rwkv_w0'] + jnp.einsum('btdr,drc->btdc', jnp.tanh(zw), p['rwkv_w2'])).astype(f32)
    decay = jnp.exp(-jnp.exp(-jax.nn.softplus(-logit) - 0.5))
    a = jax.nn.sigmoid(p['rwkv_a0'] + jnp.einsum('btdr,drc->btdc', za, p['rwkv_a2']))
    g = jax.nn.sigmoid(zg) @ p['rwkv_g2']

    def heads(t):
        return t.reshape(t.shape[:-1] + (RWKV_HEADS, RWKV_HEAD_SIZE))

    kk = heads(k * p['rwkv_k_k']).astype(f32)
    kk = kk / jnp.maximum(jnp.sqrt(jnp.sum(kk * kk, axis=-1, keepdims=True)), 1e-12)
    k_dir = heads(k[:, :, None, :] * (1.0 + (a - 1.0) * p['rwkv_k_a']))
    b_dir = heads(a) * kk[:, :, None]
    r_h, v_h = heads(r), heads(v)

    def two(t):
        return jnp.broadcast_to(t[:, :, None], (B, T, N_DIR) + t.shape[2:])

    def scan_order(t):
        return jnp.stack([t[:, :, 0], jnp.flip(t[:, :, 1], axis=1)], axis=2)

    xs = tuple(jnp.moveaxis(scan_order(t).astype(f32), 1, 0)
               for t in (heads(decay), k_dir, two(v_h), two(r_h), two(-kk), b_dir))

    def step(S, inp):
        w_t, k_t, v_t, r_t, a_t, b_t = inp
        sa = jnp.einsum('bdhij,bdhj->bdhi', S, a_t)
        S = S * w_t[..., None, :] + sa[..., None] * b_t[..., None, :] + v_t[..., None] * k_t[..., None, :]
        return S, jnp.einsum('bdhij,bdhj->bdhi', S, r_t)

    s_final, ys = lax.scan(step, state0.astype(f32), xs)
    ys = jnp.moveaxis(ys, 0, 1)
    y = ys[:, :, 0] + jnp.flip(ys[:, :, 1], axis=1)
    mean = jnp.mean(y, axis=-1, keepdims=True)
    var = jnp.mean(jnp.square(y - mean), axis=-1, keepdims=True)
    y = ((y - mean) * lax.rsqrt(var + GN_EPS)).reshape(B, T, RWKV_WIDTH) * p['rwkv_ln_g'] + p['rwkv_ln_b']
    bonus = jnp.sum(jnp.sum(r_h[:, :, None] * k_dir * p['rwkv_r_k'], axis=-1, keepdims=True), axis=2) * v_h
    out = (y + bonus.reshape(B, T, RWKV_WIDTH)) * g
    return out, s_final


def conv_ffn(h, w_up, conv_w, conv_b, w_down):
    u = h @ w_up
    prev, nxt = neighbours(u)
    u = prev * conv_w[0] + u * conv_w[1] + nxt * conv_w[2] + conv_b
    val, gate = jnp.split(u, 2, axis=-1)
    return (jax.nn.silu(gate) * val) @ w_down


def trunk_layer(x, cond, p, ctx):
    shift1, scale1, gate1, shift2, scale2, gate2 = modulation(cond, p['w_ada'], p['b_ada'])
    B, T = x.shape[:2]
    h = rms_norm(x, p['g_norm1']) * (1.0 + scale1) + shift1
    z = h @ p['w_in']
    q, k, v, ga, gb, zr = jnp.split(z, IN_SPLIT_IDX, axis=-1)
    q = q.reshape(B, T, N_Q_HEADS, HEAD_DIM)
    k = k.reshape(B, T, N_KV_HEADS, HEAD_DIM)
    v = v.reshape(B, T, N_KV_HEADS, HEAD_DIM)
    if ctx is None:
        attn = dense_context_attention(q, k, v, p['attn_sink'])
        state0 = jnp.zeros((B, N_DIR, RWKV_HEADS, RWKV_HEAD_SIZE, RWKV_HEAD_SIZE), jnp.float32)
    else:
        k_ctx, v_ctx, state0 = ctx
        k = axial_rope(k)
        attn = banded_latent_attention(axial_rope(q), k, v, k_ctx, v_ctx, p['attn_sink'])
    rw, s_final = rwkv7_bidirectional(zr, state0, p)
    merged = jax.nn.sigmoid(ga) * (attn @ p['w_proj_a']) + jax.nn.sigmoid(gb) * (rw @ p['w_proj_b'])
    x = x + gate1 * (merged @ p['w_out'])
    h2 = rms_norm(x, p['g_norm2']) * (1.0 + scale2) + shift2
    x = x + gate2 * conv_ffn(h2, p['w_ffn_up'], p['ffn_conv_w'], p['ffn_conv_b'], p['w_ffn_down'])
    return x, (k, v, s_final)


def setup_inputs(seed: int = 0) -> dict:
    key = jax.random.key(seed)
    ks = iter(jax.random.split(key, 40))
    f32 = jnp.float32

    def nrm(shape, scale):
        return jax.random.normal(next(ks), shape, f32) * scale

    L = DEPTH
    return {
        'x_prompt': nrm((BATCH, SEQ, D_MODEL), 1.0),
        'x_sample': nrm((DEC_BATCH, DEC_SEQ, D_MODEL), 1.0),
        'c': nrm((DEC_BATCH, D_MODEL), 1.0),
        'cache_k': nrm((DEC_BATCH, L, PAST_LEN, N_KV_HEADS, HEAD_DIM), 1.0),
        'cache_v': nrm((DEC_BATCH, L, PAST_LEN, N_KV_HEADS, HEAD_DIM), 1.0),
        'state_rwkv': nrm((DEC_BATCH, L, N_DIR, RWKV_HEADS, RWKV_HEAD_SIZE, RWKV_HEAD_SIZE), 0.3),
        'c_ctx': nrm((D_MODEL,), 1.0),
        'w_ada': nrm((L, D_MODEL, 6 * D_MODEL), 0.5 * D_MODEL ** -0.5),
        'b_ada': nrm((L, 6 * D_MODEL), 0.01),
        'g_norm1': 1.0 + nrm((L, D_MODEL), 0.02),
        'w_in': nrm((L, D_MODEL, W_IN_COLS), D_MODEL ** -0.5),
        'attn_sink': nrm((L, N_Q_HEADS), 0.5),
        'w_proj_a': nrm((L, ATTN_WIDTH, D_MODEL), ATTN_WIDTH ** -0.5),
        'w_proj_b': nrm((L, RWKV_WIDTH, D_MODEL), RWKV_WIDTH ** -0.5),
        'rwkv_mu': jax.random.uniform(next(ks), (L, 2, RW_SHIFT_COLS), f32, 0.0, 0.5),
        'rwkv_w0': -1.0 + nrm((L, N_DIR, RWKV_WIDTH), 0.5),
        'rwkv_w2': nrm((L, N_DIR, DECAY_RANK, RWKV_WIDTH), 0.1),
        'rwkv_a0': nrm((L, N_DIR, RWKV_WIDTH), 0.5),
        'rwkv_a2': nrm((L, N_DIR, ICLR_RANK, RWKV_WIDTH), 0.1),
        'rwkv_k_k': 0.85 + nrm((L, RWKV_WIDTH), 0.1),
        'rwkv_k_a': 1.0 + nrm((L, RWKV_WIDTH), 0.1),
        'rwkv_r_k': nrm((L, RWKV_HEADS, RWKV_HEAD_SIZE), 0.1),
        'rwkv_g2': nrm((L, GATE_RANK, RWKV_WIDTH), GATE_RANK ** -0.5),
        'rwkv_ln_g': 1.0 + nrm((L, RWKV_WIDTH), 0.02),
        'rwkv_ln_b': nrm((L, RWKV_WIDTH), 0.01),
        'w_out': nrm((L, D_MODEL, D_MODEL), D_MODEL ** -0.5),
        'g_norm2': 1.0 + nrm((L, D_MODEL), 0.02),
        'w_ffn_up': nrm((L, D_MODEL, 2 * D_FF), D_MODEL ** -0.5),
        'ffn_conv_w': nrm((L, CONV_W, 2 * D_FF), CONV_W ** -0.5),
        'ffn_conv_b': nrm((L, 2 * D_FF), 0.01),
        'w_ffn_down': nrm((L, D_FF, D_MODEL), D_FF ** -0.5),
        'g_final': 1.0 + nrm((D_MODEL,), 0.02),
    }


def reference(x_prompt, x_sample, c, cache_k, cache_v, state_rwkv, c_ctx, w_ada, b_ada, g_norm1,
              w_in, attn_sink, w_proj_a, w_proj_b, rwkv_mu, rwkv_w0, rwkv_w2, rwkv_a0, rwkv_a2,
              rwkv_k_k, rwkv_k_a, rwkv_r_k, rwkv_g2, rwkv_ln_g, rwkv_ln_b, w_out, g_norm2,
              w_ffn_up, ffn_conv_w, ffn_conv_b, w_ffn_down, g_final):
    hp, hs = x_prompt, x_sample
    new_k, new_v, new_s = [], [], []
    for l in range(DEPTH):
        p = dict(w_ada=w_ada[l], b_ada=b_ada[l], g_norm1=g_norm1[l], w_in=w_in[l],
                 attn_sink=attn_sink[l], w_proj_a=w_proj_a[l], w_proj_b=w_proj_b[l],
                 rwkv_mu=rwkv_mu[l], rwkv_w0=rwkv_w0[l], rwkv_w2=rwkv_w2[l], rwkv_a0=rwkv_a0[l],
                 rwkv_a2=rwkv_a2[l], rwkv_k_k=rwkv_k_k[l], rwkv_k_a=rwkv_k_a[l], rwkv_r_k=rwkv_r_k[l],
                 rwkv_g2=rwkv_g2[l], rwkv_ln_g=rwkv_ln_g[l], rwkv_ln_b=rwkv_ln_b[l], w_out=w_out[l],
                 g_norm2=g_norm2[l], w_ffn_up=w_ffn_up[l], ffn_conv_w=ffn_conv_w[l],
                 ffn_conv_b=ffn_conv_b[l], w_ffn_down=w_ffn_down[l])
        hp, (kc, vc, sc) = trunk_layer(hp, c_ctx[None, :], p, None)
        new_k.append(kc)
        new_v.append(vc)
        new_s.append(sc)
        hs, _ = trunk_layer(hs, c, p, (cache_k[:, l], cache_v[:, l], state_rwkv[:, l]))
    y_prompt = rms_norm(hp, g_final).astype(x_prompt.dtype)
    y_sample = rms_norm(hs, g_final).astype(x_sample.dtype)
    new_cache_k = jnp.stack(new_k, axis=1)
    new_cache_v = jnp.stack(new_v, axis=1)
    new_state_rwkv = jnp.stack(new_s, axis=1)
    return (y_prompt, y_sample, new_cache_k, new_cache_v, new_state_rwkv)
```

```python
import contextlib
import numpy as np
import concourse.bass as bass
import concourse.mybir as mybir
from concourse.bass_utils import run_bass_kernel_spmd

F32 = mybir.dt.float32
BF16 = mybir.dt.bfloat16
U8 = mybir.dt.uint8
AF = mybir.ActivationFunctionType
ALU = mybir.AluOpType
AX = mybir.AxisListType

ENGS = ("pe", "act", "dve", "pool", "sp")
NCORES = 8
D = 1024
NT = 3072
NS = 2048
WCOLS = 4736
DFF = 2816
NFC = 22
CDEC = -0.6065306597126334
NORM_EPS = 1e-6
GN_EPS = 64e-5


class Buf:
    __slots__ = ("name", "st", "excl")

    def __init__(self, name, excl=False):
        self.name = name
        self.excl = excl
        self.st = {}

    def _conf(self, part):
        if part is None:
            return list(self.st.values())
        out = []
        if part in self.st:
            out.append(self.st[part])
        if None in self.st:
            out.append(self.st[None])
        return out

    def on_read(self, op, part, deps):
        for s in self._conf(part):
            if s[0] is not None:
                deps.add(s[0])
            if self.excl:
                for r in s[1]:
                    if r.eng != op.eng:
                        deps.add(r)
        self.st.setdefault(part, [None, []])[1].append(op)

    def on_write(self, op, part, deps):
        for s in self._conf(part):
            if s[0] is not None:
                deps.add(s[0])
            deps.update(s[1])
        if part is None:
            self.st = {None: [op, []]}
        else:
            self.st[part] = [op, []]


class Op:
    __slots__ = ("eng", "fn", "deps", "dma", "sem", "val", "signals", "prewait", "idx", "hard")

    def __init__(self, eng, fn, dma):
        self.eng = eng
        self.fn = fn
        self.hard = ()
        self.deps = set()
        self.dma = dma
        self.sem = None
        self.val = 0
        self.signals = dma
        self.prewait = None
        self.idx = 0


class Prog:
    NPOOL = 8

    def __init__(self, nc):
        self.nc = nc
        self.ops = []
        self.last = {e: None for e in ENGS}
        self.dma_since_barrier = []

    @staticmethod
    def _norm(lst):
        out = []
        for x in lst:
            if x is None:
                continue
            if isinstance(x, tuple):
                out.append((x[0].buf if hasattr(x[0], "buf") else x[0], x[1]))
            else:
                out.append((x.buf if hasattr(x, "buf") else x, None))
        return out

    def add(self, eng, fn, reads=(), writes=(), dma=False, hard=()):
        op = Op(eng, fn, dma)
        op.hard = tuple(hard)
        op.deps.update(op.hard)
        op.idx = len(self.ops)
        for b, p in self._norm(reads):
            b.on_read(op, p, op.deps)
        for b, p in self._norm(writes):
            b.on_write(op, p, op.deps)
        op.deps.discard(op)
        self.ops.append(op)
        self.last[eng] = op
        if dma:
            self.dma_since_barrier.append(op)
        return op

    def barrier(self):
        lasts = [o for o in self.last.values() if o is not None]
        dmas = list(self.dma_since_barrier)
        self.dma_since_barrier = []
        for e in ENGS:
            op = Op(e, None, False)
            op.idx = len(self.ops)
            op.deps = set(lasts) | set(dmas)
            self.ops.append(op)
            self.last[e] = op

    def emit(self):
        nc = self.nc
        for op in self.ops:
            per_eng = {}
            nd = set()
            for d in op.deps:
                if d.fn is None:
                    continue
                if d.dma:
                    nd.add(d)
                    continue
                if d.eng == "pe" and op.eng == "pe" and not op.dma and d not in op.hard:
                    continue
                c = per_eng.get(d.eng)
                if c is None or d.idx > c.idx:
                    per_eng[d.eng] = d
            nd.update(per_eng.values())
            op.deps = nd
            for d in nd:
                d.signals = True
        with contextlib.ExitStack() as es:
            esem = {e: es.enter_context(nc.semaphore("sem_" + e)) for e in ENGS}
            pools = {e: [es.enter_context(nc.semaphore("dq_%s_%d" % (e, i))) for i in range(self.NPOOL)]
                     for e in ("sp", "act", "pool")}
            pool_tot = {e: [0] * self.NPOOL for e in pools}
            pool_rr = {e: 0 for e in pools}
            cnt = {e: 0 for e in ENGS}
            for op in self.ops:
                if op.fn is None:
                    continue
                if op.dma:
                    q = op.eng
                    i = pool_rr[q]
                    pool_rr[q] = (i + 1) % self.NPOOL
                    op.prewait = (pools[q][i], pool_tot[q][i])
                    pool_tot[q][i] += 16
                    op.sem = pools[q][i]
                    op.val = pool_tot[q][i]
                elif op.signals:
                    cnt[op.eng] += 1
                    op.sem = esem[op.eng]
                    op.val = cnt[op.eng]
            by_eng = {e: [o for o in self.ops if o.eng == e] for e in ENGS}
            self.stats = {e + "_ops": len([o for o in by_eng[e] if o.fn is not None]) for e in ENGS}
            self.stats.update({e + "_sig": cnt[e] for e in ENGS})
            final = []
            for q in pools:
                for i in range(self.NPOOL):
                    if pool_tot[q][i] > 0:
                        final.append((pools[q][i], pool_tot[q][i]))
            blk = es.enter_context(nc.Block())

            def run(e, ename):
                waited = {}

                def w(sem, val):
                    if sem is None or val <= 0:
                        return
                    k = id(sem)
                    if waited.get(k, 0) >= val:
                        return
                    waited[k] = val
                    self.stats[ename + "_waits"] = self.stats.get(ename + "_waits", 0) + 1
                    e.wait_ge(sem, val)

                for op in by_eng[ename]:
                    for d in sorted(op.deps, key=lambda o: o.idx):
                        w(d.sem, d.val)
                    if op.fn is None:
                        continue
                    if op.dma:
                        w(*op.prewait)
                        op.fn(e).then_inc(op.sem, 16)
                    else:
                        ins = op.fn(e)
                        if op.signals:
                            ins.then_inc(op.sem, 1)
                if ename == "sp":
                    for s, v in final:
                        w(s, v)
                    for en in ENGS:
                        if cnt[en] > 0:
                            w(esem[en], cnt[en])

            @blk.tensor
            def _(e):
                run(e, "pe")

            @blk.scalar
            def _(e):
                run(e, "act")

            @blk.vector
            def _(e):
                run(e, "dve")

            @blk.gpsimd
            def _(e):
                run(e, "pool")

            @blk.sync
            def _(e):
                run(e, "sp")
        return cnt


class T:
    def __init__(self, ap, name, excl=False):
        self.ap = ap
        self.buf = Buf(name, excl)

    def __getitem__(self, k):
        return self.ap[k]


ARENA = 204 * 1024


class KB:
    def __init__(self, debug=False, upto="C2"):
        self.debug = debug
        self.upto = upto
        self.nc = bass.Bass("TRN2", target_bir_lowering=False)
        self.P = Prog(self.nc)
        self.inp = {}
        self.out = {}
        self.scr = {}
        self.dbufs = {}
        self._rr = 0
        self._alt = 0

    def din(self, name, shape, dt=F32):
        self.inp[name] = self.nc.dram_tensor(name, list(shape), dt, kind="ExternalInput").ap()
        self.dbufs[name] = Buf("d_" + name)
        return self.inp[name]

    def dout(self, name, shape, dt=F32):
        self.out[name] = self.nc.dram_tensor(name, list(shape), dt, kind="ExternalOutput").ap()
        self.dbufs[name] = Buf("d_" + name)
        return self.out[name]

    def dscr(self, name, shape, dt=F32):
        kind = "ExternalOutput" if self.debug else "Internal"
        self.scr[name] = self.nc.dram_tensor(name, list(shape), dt, kind=kind).ap()
        self.dbufs[name] = Buf("d_" + name)
        return self.scr[name]

    def alloc(self, name, shape, dt, top=False):
        esz = 4 if dt == F32 else 2
        n = int(np.prod(shape[1:])) * esz
        assert self.off + n <= self.top, ("SBUF overflow", name, self.off, n, self.top)
        if top:
            self.top -= (n + 63) // 64 * 64
            ap = self.arena[:, self.top:self.top + n].bitcast(dt)
        else:
            ap = self.arena[:, self.off:self.off + n].bitcast(dt)
            self.off += (n + 63) // 64 * 64
        if len(shape) == 3:
            ap = ap.rearrange("p (a b) -> p a b", a=shape[1])
        elif len(shape) == 4:
            ap = ap.rearrange("p (a b c) -> p a b c", a=shape[1], b=shape[2])
        if shape[0] < 128:
            ap = ap[0:shape[0]]
        return T(ap, name)

    def bank(self):
        b = self.ps[self._rr]
        self._rr = (self._rr + 1) % 8
        return b

    def alt(self, engs=("act", "dve")):
        self._alt += 1
        return engs[self._alt % len(engs)]

    def dma(self, eng, out, in_, reads, writes):
        self.P.add(eng, lambda e: e.dma_start(out=out, in_=in_), reads=reads, writes=writes, dma=True)

    def mm(self, bank, out, lhsT, rhs, reads, start=True, stop=True, hard=()):
        return self.P.add("pe", lambda e: e.matmul(out, lhsT=lhsT, rhs=rhs, start=start, stop=stop),
                          reads=reads, writes=[bank], hard=hard)

    def mm4(self, bank, hd, fn):
        prev = None
        for u in (0, 2, 1, 3):
            prev = fn(u, (prev,) if u == 1 else ())
        return prev

    def tr(self, bank, out, in_, ident, reads):
        self.P.add("pe", lambda e: e.transpose(out=out, in_=in_, identity=ident), reads=reads, writes=[bank])

    def act(self, out, in_, func, reads, writes, scale=1.0, bias=0.0, eng="act"):
        self.P.add("act", lambda e: e.activation(out=out, in_=in_, func=func, bias=bias, scale=scale),
                   reads=reads, writes=writes)

    def tt(self, eng, out, in0, in1, op, reads, writes):
        self.P.add(eng, lambda e: e.tensor_tensor(out=out, in0=in0, in1=in1, op=op), reads=reads, writes=writes)

    def ts(self, eng, out, in0, s1, s2, op0, op1, reads, writes):
        if s2 is None:
            self.P.add(eng, lambda e: e.tensor_scalar(out=out, in0=in0, scalar1=s1, scalar2=None, op0=op0),
                       reads=reads, writes=writes)
        else:
            self.P.add(eng, lambda e: e.tensor_scalar(out=out, in0=in0, scalar1=s1, scalar2=s2, op0=op0, op1=op1),
                       reads=reads, writes=writes)

    def stt(self, eng, out, in0, scalar, in1, op0, op1, reads, writes):
        eng = "dve"
        self.P.add(eng, lambda e: e.scalar_tensor_tensor(out=out, in0=in0, scalar=scalar, in1=in1, op0=op0, op1=op1),
                   reads=reads, writes=writes)

    def copy(self, eng, out, in_, reads, writes):
        if eng == "act":
            self.P.add("act", lambda e: e.activation(out=out, in_=in_, func=AF.Copy), reads=reads, writes=writes)
        else:
            self.P.add(eng, lambda e: e.tensor_copy(out=out, in_=in_), reads=reads, writes=writes)

    def memset(self, eng, out, val, writes):
        self.P.add(eng, lambda e: e.memset(out, val), writes=writes)

    def recip(self, out, in_, reads, writes):
        self.P.add("dve", lambda e: e.reciprocal(out=out, in_=in_), reads=reads, writes=writes)

    def powneg(self, out, in_, reads, wt, expo, scale=1.0, bias=0.0):
        self.act(out, in_, AF.Ln, reads, [wt], scale=scale, bias=bias)
        self.act(out, out, AF.Exp, [wt], [wt], scale=-expo)

    def declare(self):
        di = self.din
        di("xT", [D, NT])
        di("cvec", [128, 8, 2])
        di("kcT", [128, 512])
        di("vc", [128, 4, 128])
        di("st0", [128, 512])
        di("w_ada", [D, 6 * D])
        di("b_adaT", [128, 48])
        di("gT", [128, 3, 8])
        di("w_in", [D, WCOLS])
        di("sink", [1, 8])
        di("w_pa", [64, 8, D])
        di("w_pb", [128, 4, D])
        di("w_out", [128, 8, D])
        di("w_up", [128, 8, 2 * DFF])
        di("w_dn", [128, NFC, D])
        di("convw", [128, 2 * NFC, 3])
        di("convb", [128, 2 * NFC])
        di("mu", [128, 15, 2])
        di("w0a0", [128, 2, 2, 4])
        di("w2", [128, 512])
        di("a2", [128, 512])
        di("g2", [128, 512])
        di("kvec", [128, 3, 4])
        di("lnT", [128, 2, 4])
        di("ident", [128, 128])
        di("masks", [128, 4, 128])
        di("blockones", [128, 128])
        di("sel", [128, 2, 128])
        di("rope", [128, 2, NS])
        di("permT", [128, 128])
        do = self.dout
        do("yT", [D, NT])
        do("newk", [1024, 128])
        do("newv", [1024, 128])
        do("newst", [4, 128, 512])
        ds = self.dscr
        ds("zrT", [15, 128, NT])
        ds("sgT", [16, 128, NT], BF16)
        ds("attnT", [64, 8, NT], BF16)
        ds("rwT", [4, 128, NT], BF16)
        ds("gTs", [4, 128, NT])
        ds("bvT", [4, 128, NT])
        ds("chG", [24, 128, 8, 128], BF16)
        ds("chH", [24, 128, 8, 64], BF16)
        ds("chZ", [24, 128, 8, 64])
        ds("y0", [24, 128, 512])
        ds("wcD", [24, 128, 2, 4])
        ds("x1T", [8, 128, NT])
        if self.debug:
            ds("dbg_mod", [128, 96])
            ds("dbg_q", [128, 4, NT], BF16)
            ds("dbg_k", [128, NT], BF16)
            ds("dbg_v", [128, 24 * 130], BF16)
            ds("dbg_y", [24, 128, 512])

    def build(self):
        nc = self.nc
        self.declare()
        with contextlib.ExitStack() as es:
            self.arena = es.enter_context(nc.sbuf_tensor("arena", [128, ARENA], U8))
            self.off = 0
            self.top = ARENA
            self.ps = []
            for i in range(8):
                t = es.enter_context(nc.psum_tensor("ps%d" % i, [128, 512], F32))
                self.ps.append(T(t[:, :], "ps%d" % i, excl=True))
            self.phase0()
            if self.upto != "0":
                self.phaseA()
            if self.upto not in ("0", "A"):
                self.phaseB1()
            if self.upto not in ("0", "A", "B1"):
                self.phaseB2()
            if self.upto not in ("0", "A", "B1", "B2"):
                self.phaseC1()
            if self.upto not in ("0", "A", "B1", "B2", "C1"):
                self.phaseC2()
            self.P.emit()
        return nc

    def phase0(self):
        I, B = self.inp, self.dbufs
        A = self.alloc

        def load(name, shape, src=None, dt=F32, eng="sp"):
            t = A(name, shape, dt)
            s = I[name] if src is None else src
            self.dma(eng, t.ap, s, [B[name]], [t])
            return t

        self.ident = load("ident", [128, 128])
        self.masks = load("masks", [128, 4, 128])
        self.bones = load("blockones", [128, 128])
        self.sel = load("sel", [128, 2, 128])
        self.gT = load("gT", [128, 3, 8])
        self.ones = A("ones", [128, 128], F32)
        self.memset("pool", self.ones.ap, 1.0, [self.ones])
        self.onesb = A("onesb", [128, 128], BF16)
        self.memset("pool", self.onesb.ap, 1.0, [self.onesb])
        self.identb = A("identb", [128, 128], BF16)
        self.copy("dve", self.identb.ap, self.ident.ap, [self.ident], [self.identb])
        self.esink = A("esink", [128, 8], F32)
        self.dma("sp", self.esink[64:65, :], I["sink"], [B["sink"]], [self.esink])
        self.act(self.esink[64:65, :], self.esink[64:65, :], AF.Exp, [self.esink], [self.esink])
        self.MS = A("MS", [128, 48, 2], F32)
        self.GS = A("GS", [128, 2, 8, 2], F32)
        mark = self.off
        self.markP = mark
        self.W = A("W", [128, 8, WCOLS], BF16, top=True)
        self.wbounds = [0, 768, 1792, 2816, 3840, WCOLS]
        w_in_v = I["w_in"].rearrange("(k p) c -> p k c", p=128)
        for bi_ in range(5):
            c0_, c1_ = self.wbounds[bi_], self.wbounds[bi_ + 1]
            self.dma("pool", self.W[:, :, c0_:c1_], w_in_v[:, :, c0_:c1_], [B["w_in"]], [(self.W, bi_)])
        cv = load("cvec", [128, 8, 2])
        bT = load("b_adaT", [128, 48])
        sc = A("sc", [128, 8, 2], F32)
        self.act(sc.ap, cv.ap, AF.Silu, [cv], [sc])
        accr = A("accr", [2, 6 * D], F32)
        self.memset("pool", accr.ap, 0.0, [accr])
        wa = [A("wa%d" % i, [128, 6 * D], F32) for i in range(2)]
        for k in range(8):
            buf = wa[k % 2]
            self.dma("sp", buf.ap, I["w_ada"][k * 128:(k + 1) * 128, :], [B["w_ada"]], [buf])
            for n in range(12):
                bk = self.bank()
                self.mm(bk, bk[0:2, :], sc[:, k, :], buf[:, n * 512:(n + 1) * 512], [buf, sc])
                self.tt("dve", accr[:, n * 512:(n + 1) * 512], accr[:, n * 512:(n + 1) * 512], bk[0:2, :], ALU.add,
                        [(accr, n), bk], [(accr, n)])
        acc = A("acc", [128, 96], F32)
        for q in range(2):
            bk = self.bank()
            for j in range(24):
                jj = q * 24 + j
                self.tr(bk, bk[:, 2 * j:2 * j + 2], accr[:, jj * 128:(jj + 1) * 128], self.ident[0:2, 0:2],
                        [(accr, jj // 4), self.ident])
            self.copy("act", acc[:, q * 48:(q + 1) * 48], bk[:, 0:48], [bk], [acc])
        self.tt("dve", self.MS.ap, acc.ap.rearrange("p (a b) -> p a b", b=2),
                bT.ap.unsqueeze(2).broadcast_to([128, 48, 2]), ALU.add, [acc, bT], [self.MS])
        for ni, (sv, gi) in enumerate(((1, 0), (4, 1))):
            tmp = A("gstmp%d" % ni, [128, 8, 2], F32)
            self.ts("dve", tmp.ap, self.MS[:, sv * 8:(sv + 1) * 8, :], 1.0, None, ALU.add, None, [self.MS], [tmp])
            self.tt("dve", self.GS[:, ni], tmp.ap, self.gT[:, gi, :].unsqueeze(2).broadcast_to([128, 8, 2]),
                    ALU.mult, [tmp, self.gT], [self.GS])
        if self.debug:
            self.dma("sp", self.scr["dbg_mod"], self.MS.ap.rearrange("p a b -> p (a b)"), [self.MS], [B["dbg_mod"]])
        self.P.barrier()
        self.off = mark

    def wblk(self, j):
        c = j * 128
        for i in range(5):
            if self.wbounds[i] <= c < self.wbounds[i + 1]:
                return i
        raise AssertionError(j)

    def msc(self, vec, k, w):
        return self.MS[:, vec * 8 + k, w:w + 1]

    def phaseA(self):
        I, B, S = self.inp, self.dbufs, self.scr
        A = self.alloc
        self.qT = A("qT", [128, 4, NT], BF16)
        self.kT = A("kT", [128, NT], BF16)
        self.Vg = A("Vg", [128, 24, 2, 65], BF16)
        self.memset("pool", self.Vg.ap, 1.0, [self.Vg])
        self.markA = self.off
        W = self.W
        permT = A("permT", [128, 128], F32)
        self.dma("sp", permT.ap, I["permT"], [B["permT"]], [permT])
        xg = [A("xg%d" % i, [128, 8, 512], F32) for i in range(1)]
        hT = [A("hT%d" % i, [128, 8, 512], BF16) for i in range(2)]
        rope = A("rope", [128, 2, 512], F32)
        rstd = A("rstd", [128, 512], F32)
        sq = [A("sq%d" % i, [128, 512], BF16) for i in range(4)]
        tmpx = [A("tmpx%d" % i, [128, 512], F32) for i in range(2)]
        QF = [A("QF%d" % i, [128, 512], F32) for i in range(2)]
        t12 = [A("t12_%d" % i, [128, 512], F32) for i in range(4)]
        SGst = [A("SGst%d" % i, [128, 4, 512], BF16) for i in range(2)]
        ZRst = [A("ZRst%d" % i, [128, 4, 512], F32) for i in range(2)]
        KVo = [A("KVo%d" % i, [128, 4, 128], F32) for i in range(2)]
        xTv = I["xT"].rearrange("(k p) t -> p k t", p=128)
        ones, GS = self.onesb, self.GS

        def front(g):
            w = 0 if g < 4 else 1
            x = xg[0]
            h = hT[g % 2]
            self.dma("sp", x.ap, xTv[:, :, g * 512:(g + 1) * 512], [B["xT"]], [x])
            if g < 4:
                self.dma("sp", rope.ap, I["rope"][:, :, g * 512:(g + 1) * 512], [B["rope"]], [rope])
            bk = self.bank()
            for k in range(10):
                if k < 8:
                    self.act(sq[k % 4].ap, x[:, k, :], AF.Square, [x], [sq[k % 4]])
                if k >= 2:
                    j = k - 2
                    self.mm(bk, bk.ap, ones.ap, sq[j % 4].ap, [ones, sq[j % 4]], start=(j == 0), stop=(j == 7))
            self.powneg(rstd.ap, bk.ap, [bk], rstd, 0.5, scale=1.0 / D, bias=NORM_EPS)
            for k in range(8):
                t = tmpx[k % 2]
                self.tt("dve", t.ap, x[:, k, :], rstd.ap, ALU.mult, [x, rstd], [t])
                self.act(h[:, k, :], t.ap, AF.Identity, [t, GS, self.MS], [(h, k)],
                         scale=GS[:, 0, k, w:w + 1], bias=self.msc(0, k, w))

        def proj(g, h, j):
            bk = self.bank()
            for k in range(8):
                self.mm(bk, bk.ap, W[:, k, j * 128:(j + 1) * 128], h[:, k, :], [(W, self.wblk(j)), (h, k)],
                        start=(k == 0), stop=(k == 7))
            return bk

        def rope_a(bk):
            qf = QF[self._alt % 2]
            self._alt += 1
            self.copy("act", qf.ap, bk.ap, [bk], [qf])
            return qf

        def rope_b(qf, dst, dparts):
            b2 = self.bank()
            self.mm(b2, b2.ap, permT.ap, qf.ap, [permT, qf])
            i = 0 if qf is QF[0] else 1
            t1, t2 = t12[i], t12[2 + i]
            self.tt("dve", t1.ap, qf.ap, rope[:, 0, :], ALU.mult, [qf, rope], [t1])
            self.tt("dve", t2.ap, b2.ap, rope[:, 1, :], ALU.mult, [b2, rope], [t2])
            self.tt("pool", dst, t1.ap, t2.ap, ALU.add, [t1, t2], dparts)

        front(0)
        for g in range(6):
            sample = g < 4
            h = hT[g % 2]
            tok = slice(g * 512, (g + 1) * 512)
            pend = None
            for a in range(5):
                bk = proj(g, h, a)
                if a < 4:
                    dst, dparts = self.qT[:, a, tok], [(self.qT, (a, g))]
                else:
                    dst, dparts = self.kT[:, tok], [(self.kT, g)]
                if sample:
                    qf = rope_a(bk)
                    if pend is not None:
                        rope_b(*pend)
                    pend = (qf, dst, dparts)
                else:
                    self.copy("act", dst, bk.ap, [bk], dparts)
            if pend is not None:
                rope_b(*pend)
            for which, col in (("v", 5),) + ((("k", 4),) if not sample else ()):
                bk = self.bank()
                for ti in range(4):
                    for k in range(8):
                        self.mm(bk, bk[:, ti * 128:(ti + 1) * 128], h[:, k, ti * 128:(ti + 1) * 128],
                                W[:, k, col * 128:(col + 1) * 128], [(W, self.wblk(col)), (h, k)],
                                start=(k == 0), stop=(k == 7))
                if which == "v":
                    self.copy("dve", self.Vg[:, g * 4:(g + 1) * 4, :, 0:64],
                              bk.ap.rearrange("p (t g d) -> p t g d", t=4, g=2), [bk], [(self.Vg, g)])
                if not sample:
                    kv = KVo[0 if which == "v" else 1]
                    self.copy("act", kv.ap, bk.ap.rearrange("p (t d) -> p t d", t=4), [bk], [kv])
                    dst = self.out["newv" if which == "v" else "newk"]
                    r0 = (g - 4) * 512
                    self.dma("sp", dst[r0:r0 + 512, :].rearrange("(t p) d -> p t d", p=128), kv.ap,
                             [kv], [B["newv" if which == "v" else "newk"]])
            for q4 in range(4):
                st = SGst[q4 % 2]
                for jj in range(4):
                    bk = proj(g, h, 6 + q4 * 4 + jj)
                    self.act(st[:, jj, :], bk.ap, AF.Sigmoid, [bk], [(st, jj)])
                    if q4 == 1 and jj == 0 and g < 5:
                        front(g + 1)
                self.dma("sp", S["sgT"][q4 * 4:(q4 + 1) * 4, :, tok].rearrange("c p t -> p c t"), st.ap,
                         [st], [B["sgT"]])
            for q4, (c0, c1) in enumerate(((0, 4), (4, 8), (8, 12), (12, 15))):
                st = ZRst[q4 % 2]
                for c in range(c0, c1):
                    bk = proj(g, h, 22 + c)
                    self.copy(self.alt(), st[:, c - c0, :], bk.ap, [bk], [(st, c - c0)])
                self.dma("sp", S["zrT"][c0:c1, :, tok].rearrange("c p t -> p c t"), st[:, 0:c1 - c0, :],
                         [st], [B["zrT"]])
        if self.debug:
            self.dma("sp", S["dbg_q"], self.qT.ap, [self.qT], [B["dbg_q"]])
            self.dma("sp", S["dbg_k"], self.kT.ap, [self.kT], [B["dbg_k"]])
            self.dma("sp", S["dbg_v"], self.Vg.ap.rearrange("p t g d -> p (t g d)"), [self.Vg], [B["dbg_v"]])
        self.P.barrier()
        self.off = self.markA
        self.top = ARENA

    def phaseB1(self):
        I, B, S = self.inp, self.dbufs, self.scr
        A = self.alloc
        kcf = A("kcf", [128, 512], F32)
        self.dma("sp", kcf.ap, I["kcT"], [B["kcT"]], [kcf])
        kc = A("kc", [128, 512], BF16)
        self.copy("dve", kc.ap, kcf.ap, [kcf], [kc])
        vcf = A("vcf", [128, 4, 128], F32)
        self.dma("sp", vcf.ap, I["vc"], [B["vc"]], [vcf])
        Vc = A("Vc", [128, 4, 2, 65], BF16)
        self.memset("pool", Vc.ap, 1.0, [Vc])
        self.copy("dve", Vc[:, :, :, 0:64], vcf.ap.rearrange("p t (g d) -> p t g d", g=2), [vcf, Vc], [Vc])
        mprev = A("mprev", [128, 4, 128], BF16)
        mnext = A("mnext", [128, 4, 128], BF16)
        self.copy("dve", mprev.ap, self.masks[:, 2, :].unsqueeze(1).broadcast_to([128, 4, 128]), [self.masks], [mprev])
        self.copy("dve", mnext.ap, self.masks[:, 3, :].unsqueeze(1).broadcast_to([128, 4, 128]), [self.masks], [mnext])
        PT = [A("PT%d" % i, [128, 4, 128], BF16) for i in range(14)]
        osb = [A("osb%d" % i, [64, 512], F32) for i in range(2)]
        rden = [A("rden%d" % i, [128, 512], F32) for i in range(2)]
        ast = [A("ast%d" % i, [64, 8, 128], BF16) for i in range(2)]
        pti = [0]
        items = [(qt, g) for qt in range(24) for g in range(2)]

        def keys_of(qt):
            if qt < 16:
                keys = [("s", j, (mprev if j == qt - 1 else (mnext if j == qt + 1 else None)))
                        for j in (qt - 1, qt, qt + 1) if 0 <= j < 16]
                keys += [("c", j, None) for j in range(4)]
            else:
                s0 = 16 + ((qt - 16) // 2) * 2
                keys = [("s", s0, None), ("s", s0 + 1, None)]
            return keys

        def stage_s(qt, g):
            rows = slice(g * 64, (g + 1) * 64)
            pts = []
            for (kind, j, m) in keys_of(qt):
                bk = self.bank()
                if kind == "s":
                    lhsT, lrd = self.kT[rows, j * 128:(j + 1) * 128], self.kT
                else:
                    lhsT, lrd = kc[rows, j * 128:(j + 1) * 128], kc
                self.mm(bk, bk.ap.rearrange("p (a q) -> p a q", a=4), lhsT,
                        self.qT[rows, :, qt * 128:(qt + 1) * 128], [lrd, self.qT])
                pt = PT[pti[0] % 14]
                pti[0] += 1
                self.act(pt.ap, bk.ap.rearrange("p (a q) -> p a q", a=4), AF.Exp, [bk], [pt], scale=0.125)
                if m is not None:
                    self.tt("pool", pt.ap, pt.ap, m.ap, ALU.mult, [pt, m], [pt])
                pts.append((kind, j, pt))
            return pts

        def stage_o1(idx, qt, g, pts):
            bo = self.bank()
            for i, (kind, j, pt) in enumerate(pts):
                if kind == "s":
                    lhsT, lrd = self.Vg[:, j, g, :], self.Vg
                else:
                    lhsT, lrd = Vc[:, j, g, :], Vc
                self.mm(bo, bo[0:65, :], lhsT, pt.ap.rearrange("p a q -> p (a q)"), [lrd, pt],
                        start=(i == 0), stop=(i == len(pts) - 1))
            o = osb[idx % 2]
            rd = rden[idx % 2]
            self.copy("act", o.ap, bo[0:64, :], [bo], [o])
            self.tt("dve", rd[64:65, :].rearrange("p (a q) -> p a q", a=4),
                    bo[64:65, :].rearrange("p (a q) -> p a q", a=4),
                    self.esink[64:65, g * 4:(g + 1) * 4].unsqueeze(2).broadcast_to([1, 4, 128]),
                    ALU.add, [bo, self.esink], [rd])
            self.powneg(rd[64:65, :], rd[64:65, :], [rd], rd, 1.0)

        def stage_o2(idx, qt, g):
            st = ast[qt % 2]
            o = osb[idx % 2]
            rd = rden[idx % 2]
            bb = self.bank()
            self.mm(bb, bb[0:64, :], self.ones[64:65, 0:64], rd[64:65, :], [self.ones, rd])
            self.tt("dve", st[:, g * 4:(g + 1) * 4, :], o.ap.rearrange("p (a q) -> p a q", a=4),
                    bb[0:64, :].rearrange("p (a q) -> p a q", a=4), ALU.mult, [o, bb], [(st, g)])
            if g == 1:
                self.dma("sp", S["attnT"][:, :, qt * 128:(qt + 1) * 128], st.ap, [st], [B["attnT"]])

        nxt_pts = stage_s(*items[0])
        for idx, (qt, g) in enumerate(items):
            cur_pts = nxt_pts
            if idx + 1 < len(items):
                nxt_pts = stage_s(*items[idx + 1])
            stage_o1(idx, qt, g, cur_pts)
            if idx > 0:
                stage_o2(idx - 1, *items[idx - 1])
        stage_o2(len(items) - 1, *items[-1])
        self.P.barrier()
        self.off = self.markP

    def phaseB2(self):
        I, B, S = self.inp, self.dbufs, self.scr
        A = self.alloc

        def load(name, shape):
            t = A(name, shape, F32)
            self.dma("sp", t.ap, I[name], [B[name]], [t])
            return t

        mu = load("mu", [128, 15, 2])
        w0a0 = load("w0a0", [128, 2, 2, 4])
        w2 = load("w2", [128, 512])
        a2 = load("a2", [128, 512])
        g2 = load("g2", [128, 512])
        kvec = load("kvec", [128, 3, 4])
        lnT = load("lnT", [128, 2, 4])
        c0 = A("c0", [128, 15], F32)
        self.tt("dve", c0.ap, mu[:, :, 0], mu[:, :, 1], ALU.add, [mu], [c0])
        self.ts("dve", c0.ap, c0.ap, -1.0, 1.0, ALU.mult, ALU.add, [c0], [c0])
        omka = A("omka", [128, 4], F32)
        self.ts("dve", omka.ap, kvec[:, 1, :], -1.0, 1.0, ALU.mult, ALU.add, [kvec], [omka])
        nmask = A("nmask", [128, 2, 128], F32)
        self.ts("dve", nmask.ap, self.masks[:, 0:2, :], -1.0, None, ALU.mult, None, [self.masks], [nmask])
        selb = A("selb", [128, 2, 128], BF16)
        self.copy("dve", selb.ap, self.sel.ap, [self.sel], [selb])
        markB = self.off
        seqs = [(0, getattr(self, "b2_sample_t", 2048))] + [(2048 + 256 * i, 256) for i in range(4)]
        only = getattr(self, "b2_only", None)
        ident, identb, masks, sel, bones, ones = self.ident, self.identb, self.masks, self.sel, self.bones, self.ones

        def m4(t, i):
            return t[:, i, :].unsqueeze(1).broadcast_to([128, 4, 128])

        I4 = ident.ap.unsqueeze(1).broadcast_to([128, 4, 128])
        blocks = []
        for si, (s0, Tn) in enumerate(seqs):
            if only is not None and si not in only:
                continue
            for b in range(Tn // 256):
                blocks.append((si, s0, Tn, b))

        ZR = A("ZR", [128, 15, 258], F32)
        ZS = A("ZS", [128, 15, 256], F32)
        TW = A("TW", [128, 256], F32)
        SGz = A("SGz", [128, 256], F32)
        kk = A("kk", [128, 4, 256], F32)
        shp = [A("shp%d" % i, [128, 256], F32) for i in range(4)]
        r1 = {n: A("r_" + n, [128, 256], F32) for n in ("kq", "sq", "nrm", "bs", "bt", "rr")}
        rd = [{n: A("r%d_%s" % (d, n), [128, 256], F32)
               for n in ("tk", "KD", "BD", "CL", "EX", "Wm", "Wi", "Wx")} for d in range(2)]
        CT = A("CT", [128, 2, 2], F32)
        Aall = [A("Aall%d" % d, [128, 4, 256], F32) for d in range(2)]
        SGall = [A("SGall%d" % d, [128, 4, 256], F32) for d in range(2)]
        PB = []
        for i in range(2):
            pb = {}
            for n in ("RT", "KT", "BT", "AT"):
                pb[n] = [A("%s%d_%d" % (n, d, i), [128, 4, 256], BF16) for d in range(2)]
            pb["WC"] = [A("WC%d_%d" % (d, i), [128, 2, 4], F32) for d in range(2)]
            pb["VT"] = A("VT_%d" % i, [128, 2, 512], BF16)
            PB.append(pb)
        Gst = A("Gst", [128, 4, 256], F32)
        BVst = A("BVst", [128, 4, 256], F32)
        U12 = [[{n: A("u%d%d_%s" % (hg, d, n), [128, 4, 128], BF16) for n in ("PA", "PB", "PTA", "PTB", "QA", "QB")}
                for d in range(2)] for hg in range(2)]
        UU = []
        for hg_ in range(2):
            U_ = []
            for d in range(2):
                u = {n: A("u%d%d_%s" % (hg_, d, n), [128, 4, 128], BF16)
                     for n in ("AkT", "RHS", "X", "Ark", "Arb", "KBp", "BBp")}
                u["CG"] = A("u%d%d_CG" % (hg_, d), [128, 4, 128], BF16)
                u["CH"] = A("u%d%d_CH" % (hg_, d), [128, 4, 64], BF16)
                u["CZ"] = A("u%d%d_CZ" % (hg_, d), [128, 4, 64], F32)
                self.memset("pool", u["KBp"].ap, 0.0, [u["KBp"]])
                self.memset("pool", u["BBp"].ap, 0.0, [u["BBp"]])
                U_.append(u)
            UU.append(U_)
        Y0st = [A("Y0st%d" % i, [128, 512], F32) for i in range(2)]

        def prep_front(bi):
            si, s0, Tn, b = blocks[bi]
            nblk = Tn // 256
            pb = PB[bi % 2]
            t0 = s0 + 256 * b
            clo = 0 if b > 0 else 1
            chi = 258 if b < nblk - 1 else 257
            if clo == 1:
                self.memset("pool", ZR[:, :, 0:1], 0.0, [ZR])
            if chi == 257:
                self.memset("pool", ZR[:, :, 257:258], 0.0, [ZR])
            self.dma("sp", ZR[:, :, clo:chi],
                     S["zrT"][:, :, t0 - 1 + clo:t0 - 1 + chi].rearrange("c p t -> p c t"), [B["zrT"]], [ZR])
            for c in range(15):
                p0, p2 = shp[(c % 2) * 2], shp[(c % 2) * 2 + 1]
                self.act(p0.ap, ZR[:, c, 0:256], AF.Copy, [ZR, mu], [p0], scale=mu[:, c, 0:1])
                self.act(p2.ap, ZR[:, c, 2:258], AF.Copy, [ZR, mu], [p2], scale=mu[:, c, 1:2])
                self.ts("dve", ZS[:, c, :], ZR[:, c, 1:257], c0[:, c:c + 1], None, ALU.mult, None, [ZR, c0], [(ZS, c)])
                self.tt("pool", p0.ap, p0.ap, p2.ap, ALU.add, [p0, p2], [p0])
                self.tt("dve", ZS[:, c, :], ZS[:, c, :], p0.ap, ALU.add, [(ZS, c), p0], [(ZS, c)])
                yield
            self.act(TW.ap, ZS[:, 12, :], AF.Tanh, [(ZS, 12)], [TW])
            self.act(SGz.ap, ZS[:, 14, :], AF.Sigmoid, [(ZS, 14)], [SGz])
            yield
            for ci in range(2):
                bk = self.bank()
                for fc in range(4):
                    self.tr(bk, bk[:, fc * 128:(fc + 1) * 128], ZS[:, 8 + fc, ci * 128:(ci + 1) * 128], ident.ap,
                            [(ZS, 8 + fc), ident])
                self.copy("act" if ci == 0 else "dve", pb["VT"][:, ci, :], bk.ap, [bk], [(pb["VT"], ci)])
            yield
            for d in range(2):
                rows = slice(d * 64, (d + 1) * 64)
                for (wmat, src, dst, wi) in ((a2, ZS[rows, 13, :], Aall, 1), (w2, TW[rows, :], SGall, 0)):
                    yield
                    for f2 in range(2):
                        bk = self.bank()
                        for q in range(2):
                            fc = 2 * f2 + q
                            self.mm(bk, bk[:, q * 256:(q + 1) * 256], wmat[rows, fc * 128:(fc + 1) * 128], src,
                                    [wmat, (ZS, 13), TW])
                        for q in range(2):
                            fc = 2 * f2 + q
                            self.act(dst[d][:, fc, :], bk[:, q * 256:(q + 1) * 256], AF.Sigmoid, [bk, w0a0],
                                     [(dst[d], fc)], bias=w0a0[:, wi, d, fc:fc + 1])

        def prep_fc(bi, fc):
            si, s0, Tn, b = blocks[bi]
            pb = PB[bi % 2]
            t0 = s0 + 256 * b
            cgb = s0 // 128 + 2 * b
            RT, KT, BT, AT, WC = (pb[n] for n in ("RT", "KT", "BT", "AT", "WC"))
            bk = self.bank()
            self.mm(bk, bk[:, 0:256], g2[:, fc * 128:(fc + 1) * 128], SGz.ap, [g2, SGz])
            self.copy("act", Gst[:, fc, :], bk[:, 0:256], [bk], [(Gst, fc)])
            kq, sqt, nrm = r1["kq"], r1["sq"], r1["nrm"]
            self.act(kq.ap, ZS[:, 4 + fc, :], AF.Copy, [(ZS, 4 + fc), kvec], [kq], scale=kvec[:, 0, fc:fc + 1])
            self.act(sqt.ap, ZS[:, 4 + fc, :], AF.Square, [(ZS, 4 + fc), kvec], [sqt], scale=kvec[:, 0, fc:fc + 1])
            yield
            bk = self.bank()
            self.mm(bk, bk[:, 0:256], bones.ap, sqt.ap, [bones, sqt])
            self.powneg(nrm.ap, bk[:, 0:256], [bk], nrm, 0.5, bias=1e-30)
            self.tt("dve", kk[:, fc, :], kq.ap, nrm.ap, ALU.mult, [kq, nrm], [(kk, fc)])
            for d in range(2):
                yield
                R = rd[d]
                Ad = Aall[d][:, fc, :]
                sgd = SGall[d][:, fc, :]
                self.act(R["tk"].ap, Ad, AF.Identity, [(Aall[d], fc), kvec, omka], [R["tk"]],
                         scale=kvec[:, 1, fc:fc + 1], bias=omka[:, fc:fc + 1])
                self.tt("pool", R["KD"].ap, R["tk"].ap, ZS[:, 4 + fc, :], ALU.mult, [R["tk"], (ZS, 4 + fc)], [R["KD"]])
                self.tt("pool", R["BD"].ap, Ad, kk[:, fc, :], ALU.mult, [(Aall[d], fc), (kk, fc)], [R["BD"]])
                for ci in range(2):
                    cs = slice(ci * 128, (ci + 1) * 128)
                    self.P.add("dve", (lambda o, d1: (lambda e: e.tensor_tensor_scan(
                        out=o, data0=ones.ap, data1=d1, initial=0.0, op0=ALU.mult, op1=ALU.add)))(
                        R["CL"][:, cs], SGall[d][:, fc, cs]), reads=[ones, (SGall[d], fc)], writes=[R["CL"]])
                self.tt("pool", R["EX"].ap, R["CL"].ap, sgd, ALU.subtract, [R["CL"], (SGall[d], fc)], [R["EX"]])
                yield
                tot = R["CL"][:, 127:256:128]
                if d == 0:
                    self.act(R["Wm"].ap, R["CL"].ap, AF.Exp, [R["CL"]], [R["Wm"]], scale=CDEC)
                    self.act(R["Wi"].ap, R["CL"].ap, AF.Exp, [R["CL"]], [R["Wi"]], scale=-CDEC)
                    self.act(R["Wx"].ap, R["EX"].ap, AF.Exp, [R["EX"]], [R["Wx"]], scale=CDEC)
                    self.act(WC[0][:, :, fc], tot, AF.Exp, [R["CL"]], [(WC[0], fc)], scale=CDEC)
                else:
                    self.ts("dve", CT[:, 0, :], tot, CDEC, None, ALU.mult, None, [R["CL"]], [CT])
                    self.ts("dve", CT[:, 1, :], tot, -CDEC, None, ALU.mult, None, [R["CL"]], [CT])
                    for ci in range(2):
                        cs = slice(ci * 128, (ci + 1) * 128)
                        self.act(R["Wm"][:, cs], R["EX"][:, cs], AF.Exp, [R["EX"], CT], [R["Wm"]],
                                 scale=-CDEC, bias=CT[:, 0, ci:ci + 1])
                        self.act(R["Wi"][:, cs], R["EX"][:, cs], AF.Exp, [R["EX"], CT], [R["Wi"]],
                                 scale=CDEC, bias=CT[:, 1, ci:ci + 1])
                        self.act(R["Wx"][:, cs], R["CL"][:, cs], AF.Exp, [R["CL"], CT], [R["Wx"]],
                                 scale=-CDEC, bias=CT[:, 0, ci:ci + 1])
                    self.act(WC[1][:, :, fc], CT[:, 0, :], AF.Exp, [CT], [(WC[1], fc)])
                yield
                self.tt("dve", RT[d][:, fc, :], ZS[:, fc, :], R["Wm"].ap, ALU.mult, [(ZS, fc), R["Wm"]], [(RT[d], fc)])
                self.tt("pool", KT[d][:, fc, :], R["KD"].ap, R["Wi"].ap, ALU.mult, [R["KD"], R["Wi"]], [(KT[d], fc)])
                self.tt("dve", BT[d][:, fc, :], R["BD"].ap, R["Wi"].ap, ALU.mult, [R["BD"], R["Wi"]], [(BT[d], fc)])
                self.tt("pool", AT[d][:, fc, :], kk[:, fc, :], R["Wx"].ap, ALU.mult, [(kk, fc), R["Wx"]], [(AT[d], fc)])
            self.tt("pool", r1["bs"].ap, rd[0]["KD"].ap, rd[1]["KD"].ap, ALU.add, [rd[0]["KD"], rd[1]["KD"]], [r1["bs"]])
            self.act(r1["rr"].ap, ZS[:, fc, :], AF.Copy, [(ZS, fc), kvec], [r1["rr"]], scale=kvec[:, 2, fc:fc + 1])
            self.tt("dve", r1["bt"].ap, r1["bs"].ap, r1["rr"].ap, ALU.mult, [r1["bs"], r1["rr"]], [r1["bt"]])
            yield
            bk = self.bank()
            self.mm(bk, bk[:, 0:256], bones.ap, r1["bt"].ap, [bones, r1["bt"]])
            self.tt("dve", BVst[:, fc, :], bk[:, 0:256], ZS[:, 8 + fc, :], ALU.mult, [bk, (ZS, 8 + fc)], [(BVst, fc)])
            if fc == 3:
                tokb = slice(t0, t0 + 256)
                self.dma("sp", S["gTs"][:, :, tokb].rearrange("c p t -> p c t"), Gst.ap, [Gst], [B["gTs"]])
                self.dma("sp", S["bvT"][:, :, tokb].rearrange("c p t -> p c t"), BVst.ap, [BVst], [B["bvT"]])
                ncs = Tn // 128
                for d in range(2):
                    for ci in range(2):
                        cl = 2 * b + ci
                        cgx = s0 // 128 + (cl if d == 0 else ncs - 1 - cl)
                        for h2 in range(2):
                            self.dma("sp", S["wcD"][cgx, d * 64:(d + 1) * 64, h2, :],
                                     WC[d][h2 * 64:(h2 + 1) * 64, ci, :], [WC[d]], [B["wcD"]])

        def v4(bk):
            return bk.ap.rearrange("p (u m) -> p u m", u=4)

        def unit_quad(bi, ci):
            si, s0, Tn, b = blocks[bi]
            pb = PB[bi % 2]
            RT, KT, BT, AT, VT = (pb[n] for n in ("RT", "KT", "BT", "AT", "VT"))
            cg = s0 // 128 + 2 * b + ci
            cs = slice(ci * 128, (ci + 1) * 128)
            y0s = Y0st[ci]

            def hd(hg, u):
                h = 4 * hg + u
                return h, h // 2, slice((u % 2) * 64, (u % 2) * 64 + 64)
            mk = [((nmask, 0), (nmask, 1), (nmask, 1), (masks, 3)), ((nmask, 1), (nmask, 0), (nmask, 0), (masks, 2))]
            combos = [(hg, d) for hg in range(2) for d in range(2)]
            for (hg, d) in combos:
                u_ = U12[hg][d]
                for nm, XA_, XB_, mi in (("PA", AT, BT, 0), ("PTA", BT, AT, 1)):
                    bL = self.bank()

                    def f_(u, hard, bL=bL, d=d, hg=hg, XA_=XA_, XB_=XB_):
                        h, fc, R = hd(hg, u)
                        return self.mm(bL, bL[:, u * 128:(u + 1) * 128], XA_[d][R, fc, cs], XB_[d][R, fc, cs],
                                       [(AT[d], fc), (BT[d], fc)], hard=hard)
                    self.mm4(bL, None, f_)
                    mt, mi_ = mk[d][mi]
                    self.tt("dve", u_[nm].ap, v4(bL), m4(mt, mi_), ALU.mult, [bL, mt], [u_[nm]])
                self.tt("pool", u_["QA"].ap, u_["PTA"].ap, I4, ALU.add, [u_["PTA"], ident], [u_["QA"]])
            next(self._filler, None)
            cur = {c: ("PA", "PTA", "QA") for c in combos}
            nxt = {"PA": "PB", "PB": "PA", "PTA": "PTB", "PTB": "PTA", "QA": "QB", "QB": "QA"}
            def b3(hg, d):
                u_ = U12[hg][d]
                pn, ptn, qn = cur[(hg, d)]
                Pn_, Qc, Qn = u_[pn], u_[qn], u_[nxt[qn]]
                bq = self.bank()
                for u in range(4):
                    self.mm(bq, bq[:, u * 128:(u + 1) * 128], Pn_[:, u, :], Qc[:, u, :], [Pn_, Qc])
                self.tt("dve", Qn.ap, v4(bq), Qc.ap, ALU.add, [bq, Qc], [Qn])
                cur[(hg, d)] = (pn, ptn, nxt[qn])

            for step in range(6):
                for qi, (hg, d) in enumerate(combos):
                    if step > 0:
                        b3(hg, d)
                    u_ = U12[hg][d]
                    pn, ptn, qn = cur[(hg, d)]
                    Pc, PTc = u_[pn], u_[ptn]
                    Pn, PTn = u_[nxt[pn]], u_[nxt[ptn]]
                    b1 = self.bank()
                    for u in range(4):
                        self.mm(b1, b1[:, u * 128:(u + 1) * 128], PTc[:, u, :], Pc[:, u, :], [PTc, Pc])
                    self.copy("act", Pn.ap, v4(b1), [b1], [Pn])
                    if step < 5:
                        b2 = self.bank()
                        for u in range(4):
                            self.mm(b2, b2[:, u * 128:(u + 1) * 128], Pc[:, u, :], PTc[:, u, :], [PTc, Pc])
                        self.copy("act" if qi % 2 == 0 else "dve", PTn.ap, v4(b2), [b2], [PTn])
                    cur[(hg, d)] = (nxt[pn], nxt[ptn] if step < 5 else ptn, qn)
                    next(self._filler, None)
            for (hg, d) in combos:
                b3(hg, d)
            next(self._filler, None)
            def s_ak(hg, d):
                u_ = UU[hg][d]
                bk = self.bank()

                def f_(u, hard, bk=bk, d=d, hg=hg):
                    h, fc, R = hd(hg, u)
                    return self.mm(bk, bk[:, u * 128:(u + 1) * 128], KT[d][R, fc, cs], AT[d][R, fc, cs],
                                   [(AT[d], fc), (KT[d], fc)], hard=hard)
                self.mm4(bk, None, f_)
                mt, mi_ = mk[d][2]
                self.tt("dve", u_["AkT"].ap, v4(bk), m4(mt, mi_), ALU.mult, [bk, mt], [u_["AkT"]])

            def s_ar(hg, d):
                u_ = UU[hg][d]
                for nm, SRC in (("Ark", KT), ("Arb", BT)):
                    bk = self.bank()

                    def f_(u, hard, bk=bk, d=d, hg=hg, SRC=SRC):
                        h, fc, R = hd(hg, u)
                        return self.mm(bk, bk[:, u * 128:(u + 1) * 128], SRC[d][R, fc, cs], RT[d][R, fc, cs],
                                       [(SRC[d], fc), (RT[d], fc)], hard=hard)
                    self.mm4(bk, None, f_)
                    mt, mi_ = mk[d][3]
                    self.tt("dve", u_[nm].ap, v4(bk), m4(mt, mi_), ALU.mult, [bk, mt], [u_[nm]])

            def s_tra(hg, d):
                u_ = UU[hg][d]
                ac = slice(d * 64, (d + 1) * 64)
                btr = self.bank()
                btb = btr.ap.bitcast(BF16)
                for f2 in range(2):
                    fc = 2 * hg + f2
                    self.tr(btr, btb[:, f2 * 128:(f2 + 1) * 128], AT[d][:, fc, cs], identb.ap, [(AT[d], fc), identb])
                self.act(u_["RHS"][:, :, ac], btb[:, 0:256].rearrange("p (u i) -> p u i", u=4), AF.Copy,
                         [btr], [u_["RHS"]], scale=-1.0)

            def s_trkb(hg, d):
                u_ = UU[hg][d]
                bk = self.bank()
                bkb = bk.ap.bitcast(BF16)
                for f2 in range(2):
                    fc = 2 * hg + f2
                    self.tr(bk, bkb[:, f2 * 128:(f2 + 1) * 128], KT[d][:, fc, cs], identb.ap, [(KT[d], fc), identb])
                    self.tr(bk, bkb[:, 256 + f2 * 128:256 + (f2 + 1) * 128], BT[d][:, fc, cs], identb.ap,
                            [(BT[d], fc), identb])
                dsl = slice(d * 64, (d + 1) * 64)
                self.copy("act", u_["KBp"][:, :, dsl], bkb[:, 0:256].rearrange("p (u i) -> p u i", u=4),
                          [bk], [u_["KBp"]])
                self.copy("dve", u_["BBp"][:, :, dsl], bkb[:, 256:512].rearrange("p (u i) -> p u i", u=4),
                          [bk], [u_["BBp"]])

            def s_av(hg, d):
                u_ = UU[hg][d]
                avc = slice((1 - d) * 64, (2 - d) * 64)
                bav = self.bank()
                for u in range(4):
                    h, fc, R = hd(hg, u)
                    self.mm(bav, bav[:, u * 64:(u + 1) * 64], u_["AkT"][:, u, :], VT[:, ci, h * 64:(h + 1) * 64],
                            [u_["AkT"], (VT, ci)])
                self.copy("act", u_["RHS"][:, :, avc], bav[:, 0:256].rearrange("p (u i) -> p u i", u=4),
                          [bav], [u_["RHS"]])

            def s_x(hg, d):
                u_ = UU[hg][d]
                Qf = U12[hg][d][cur[(hg, d)][2]]
                bx = self.bank()
                for u in range(4):
                    self.mm(bx, bx[:, u * 128:(u + 1) * 128], Qf[:, u, :], u_["RHS"][:, u, :], [Qf, u_["RHS"]])
                self.copy("dve" if d == 0 else "act", u_["X"].ap, v4(bx), [bx], [u_["X"]])

            def s_y0(hg, d):
                u_ = UU[hg][d]
                avc = slice((1 - d) * 64, (2 - d) * 64)
                by = self.bank()
                for u in range(4):
                    h, fc, R = hd(hg, u)
                    self.mm(by, by[:, u * 64:(u + 1) * 64], u_["Ark"][:, u, :], VT[:, ci, h * 64:(h + 1) * 64],
                            [u_["Ark"], (VT, ci)], start=True, stop=False)
                    self.mm(by, by[:, u * 64:(u + 1) * 64], u_["Arb"][:, u, :], u_["X"][:, u, avc],
                            [u_["Arb"], u_["X"]], start=False, stop=True)
                ysl = y0s[:, hg * 256:(hg + 1) * 256]
                if d == 0:
                    self.copy("act", ysl, by[:, 0:256], [by], [(y0s, hg)])
                else:
                    self.tt("dve", ysl, by[:, 0:256], ysl, ALU.add, [by, (y0s, hg)], [(y0s, hg)])

            def s_g(hg, d):
                u_ = UU[hg][d]
                dsl = slice(d * 64, (d + 1) * 64)
                cgx = cg if d == 0 else (s0 // 128 + Tn // 128 - 1 - (2 * b + ci))
                bg = self.bank()
                for u in range(4):
                    h, fc, R = hd(hg, u)
                    self.mm(bg, bg[:, u * 128:(u + 1) * 128], u_["X"][:, u, :], u_["Arb"][:, u, :],
                            [u_["X"], u_["Arb"]], start=True, stop=False)
                    self.mm(bg, bg[:, u * 128:(u + 1) * 128], selb[R, d, :], RT[d][R, fc, cs],
                            [selb, (RT[d], fc)], start=False, stop=True)
                self.copy("act", u_["CG"][dsl], bg[dsl, :].rearrange("p (u m) -> p u m", u=4), [bg], [u_["CG"]])
                self.dma("sp", S["chG"][cgx, dsl, 4 * hg:4 * hg + 4, :], u_["CG"][dsl], [u_["CG"]], [B["chG"]])

            def s_hz(hg, d):
                u_ = UU[hg][d]
                dsl = slice(d * 64, (d + 1) * 64)
                avc = slice((1 - d) * 64, (2 - d) * 64)
                cgx = cg if d == 0 else (s0 // 128 + Tn // 128 - 1 - (2 * b + ci))
                bh = self.bank()
                for u in range(4):
                    h, fc, R = hd(hg, u)
                    self.mm(bh, bh[:, u * 64:(u + 1) * 64], u_["X"][:, u, :], u_["BBp"][:, u, dsl],
                            [u_["X"], u_["BBp"]], start=True, stop=False)
                    self.mm(bh, bh[:, u * 64:(u + 1) * 64], selb[R, d, :], identb[R, R],
                            [selb, identb], start=False, stop=True)
                for u in range(4):
                    h, fc, R = hd(hg, u)
                    self.mm(bh, bh[:, 256 + u * 64:256 + (u + 1) * 64], u_["KBp"][:, u, :],
                            VT[:, ci, h * 64:(h + 1) * 64], [u_["KBp"], (VT, ci)], start=True, stop=False)
                    self.mm(bh, bh[:, 256 + u * 64:256 + (u + 1) * 64], u_["BBp"][:, u, :], u_["X"][:, u, avc],
                            [u_["BBp"], u_["X"]], start=False, stop=True)
                self.copy("dve", u_["CH"][dsl], bh[dsl, 0:256].rearrange("p (u m) -> p u m", u=4), [bh], [u_["CH"]])
                self.copy("act", u_["CZ"][dsl], bh[dsl, 256:512].rearrange("p (u m) -> p u m", u=4), [bh], [u_["CZ"]])
                self.dma("sp", S["chH"][cgx, dsl, 4 * hg:4 * hg + 4, :], u_["CH"][dsl], [u_["CH"]], [B["chH"]])
                self.dma("sp", S["chZ"][cgx, dsl, 4 * hg:4 * hg + 4, :], u_["CZ"][dsl], [u_["CZ"]], [B["chZ"]])

            for stg in (s_ak, s_tra, s_ar, s_trkb, s_av, s_x, s_y0, s_g, s_hz):
                for hg in range(2):
                    for d in range(2):
                        stg(hg, d)
                    if stg in (s_ar, s_x, s_g):
                        next(self._filler, None)
            self.dma("sp", S["y0"][cg], y0s.ap, [y0s], [B["y0"]])

        nb = len(blocks)

        def prep_gen(bi):
            yield from prep_front(bi)
            for fc in range(4):
                yield
                yield from prep_fc(bi, fc)

        for _ in prep_gen(0):
            pass
        for bi in range(nb):
            self._filler = prep_gen(bi + 1) if bi + 1 < nb else iter(())
            for ci in range(2):
                unit_quad(bi, ci)
            for _ in self._filler:
                pass

        self.P.barrier()
        self.off = markB
        lo, hi = slice(0, 64), slice(64, 128)
        Yacc_all = A("Yacc", [128, 16, 512], F32)
        St = [A("St%d" % i, [128, 512], F32) for i in range(2)]
        G2 = [A("G2_%d" % i, [128, 8, 128], BF16) for i in range(2)]
        HBD = [A("HBD%d" % i, [128, 8, 128], BF16) for i in range(2)]
        Stb = [A("Stb%d" % i, [128, 512], BF16) for i in range(2)]
        Z2 = [A("Z2_%d" % i, [128, 512], F32) for i in range(2)]
        WCc = [A("WCc%d" % i, [128, 2, 4], F32) for i in range(2)]
        tsum = A("tsum", [128, 512], F32)
        for i in range(2):
            self.memset("pool", HBD[i].ap, 0.0, [HBD[i]])
        YaccP = A("YaccP", [128, 8, 512], F32)
        st32 = [A("st32_%d" % i, [128, 4, 32], F32) for i in range(2)]
        Dv = [A("Dv%d" % i, [128, 32, 64], F32) for i in range(2)]
        SQ = A("SQ", [128, 32, 64], F32)
        RW1 = [A("RW1_%d" % i, [128, 4, 512], F32) for i in range(2)]
        BVl = [A("BVl%d" % i, [128, 4, 512], F32) for i in range(2)]
        Gl = [A("Gl%d" % i, [128, 4, 512], F32) for i in range(2)]
        RWo = [A("RWo%d" % i, [128, 4, 512], BF16) for i in range(2)]
        post_groups = []
        for si, (s0, Tn) in enumerate(seqs):
            if only is not None and si not in only:
                continue
            NC = Tn // 128
            cg0 = s0 // 128
            if si == 0:
                Yacc = T(Yacc_all[:, 0:NC, :], "Yacc_v")
                Yacc.buf = Yacc_all.buf
                for c0_ in range(0, NC, 4):
                    post_groups.append((Yacc_all, c0_, min(4, NC - c0_), s0 + c0_ * 128))
            else:
                Yacc = T(YaccP[:, 2 * (si - 1):2 * si, :], "YaccP_v%d" % si)
                post_groups.append((Yacc, 0, 2, s0))
            self.dma("sp", Yacc.ap, S["y0"][cg0:cg0 + NC].rearrange("c p f -> p c f"), [B["y0"]], [Yacc])
            if si == 0:
                self.dma("sp", St[0].ap, I["st0"], [B["st0"]], [St[0]])
            else:
                self.memset("pool", St[0].ap, 0.0, [St[0]])
            self.copy("act", Stb[0].ap, St[0].ap, [St[0]], [Stb[0]])
            for k in range(NC):
                cf, cb = cg0 + k, cg0 + NC - 1 - k
                g2_, hb, z2, wcc = G2[k % 2], HBD[k % 2], Z2[k % 2], WCc[k % 2]
                sc_, sn = Stb[k % 2], St[(k + 1) % 2]
                self.dma("sp", g2_.ap, S["chG"][cf], [B["chG"]], [g2_])
                self.dma("sp", hb[lo, :, 0:64], S["chH"][cf, lo], [B["chH"]], [(hb, 0)])
                self.dma("sp", hb[hi, :, 64:128], S["chH"][cf, hi], [B["chH"]], [(hb, 1)])
                self.dma("sp", z2.ap.rearrange("p (h i) -> p h i", h=8), S["chZ"][cf], [B["chZ"]], [z2])
                self.dma("sp", wcc.ap, S["wcD"][cf], [B["wcD"]], [wcc])
                bs = self.bank()
                for h in range(8):
                    self.mm(bs, bs[:, h * 64:(h + 1) * 64], hb[:, h, :], sc_[:, h * 64:(h + 1) * 64], [hb, sc_])
                self.tt("dve", tsum.ap, bs.ap, z2.ap, ALU.add, [bs, z2], [tsum])
                self.tt("dve", sn.ap.rearrange("p (f q i) -> p f q i", f=4, q=2),
                        tsum.ap.rearrange("p (f q i) -> p f q i", f=4, q=2),
                        wcc.ap.rearrange("p q f -> p f q").unsqueeze(3).broadcast_to([128, 4, 2, 64]), ALU.mult,
                        [tsum, wcc], [sn])
                if k + 1 < NC:
                    self.copy("act", Stb[(k + 1) % 2].ap, sn.ap, [sn], [Stb[(k + 1) % 2]])
                for (half, cc) in ((lo, k), (hi, NC - 1 - k)):
                    by = self.bank()
                    for h in range(8):
                        self.mm(by, by[:, h * 64:(h + 1) * 64], g2_[half, h, :], sc_[half, h * 64:(h + 1) * 64], [g2_, sc_])
                    self.tt("pool" if False else "dve", Yacc[:, cc, :], by.ap, Yacc[:, cc, :], ALU.add,
                            [by, (Yacc, cc)], [(Yacc, cc)])
            if si > 0:
                self.dma("sp", self.out["newst"][si - 1], St[NC % 2].ap, [St[NC % 2]], [B["newst"]])
            if self.debug:
                self.dma("sp", S["dbg_y"][cg0:cg0 + NC].rearrange("c p f -> p c f"), Yacc.ap, [Yacc], [B["dbg_y"]])
        pg = [g for g in post_groups if g[0] is Yacc_all]
        pp = [g for g in post_groups if g[0] is not Yacc_all]
        if len(pp) == 4:
            pg += [(YaccP, 0, 4, 2048), (YaccP, 4, 4, 2048 + 512)]
        else:
            pg += pp
        for gi, (Ysrc, c0_, n, tok0) in enumerate(pg):
            nh = n * 8
            tokc = slice(tok0, tok0 + n * 128)
            Y = Ysrc[:, c0_:c0_ + n, :].rearrange("p c (h i) -> p (c h) i", h=8)
            dv, rw1, bvl, gl, rwo, s8 = Dv[gi % 2], RW1[gi % 2], BVl[gi % 2], Gl[gi % 2], RWo[gi % 2], st32[gi % 2]
            self.dma("sp", bvl[:, :, 0:n * 128], S["bvT"][:, :, tokc].rearrange("c p t -> p c t"), [B["bvT"]], [bvl])
            self.dma("sp", gl[:, :, 0:n * 128], S["gTs"][:, :, tokc].rearrange("c p t -> p c t"), [B["gTs"]], [gl])
            self.P.add("dve", (lambda o, i_: (lambda e: e.tensor_reduce(out=o, in_=i_, axis=AX.X, op=ALU.add)))(
                s8[:, 0, 0:nh], Y), reads=[Ysrc], writes=[s8])
            self.ts("dve", s8[:, 1, 0:nh], s8[:, 0, 0:nh], -1.0 / 64, None, ALU.mult, None, [s8], [s8])
            self.tt("dve", dv[:, 0:nh, :], Y, s8[:, 1, 0:nh].unsqueeze(2).broadcast_to([128, nh, 64]), ALU.add,
                    [Ysrc, s8], [dv])
            self.act(SQ[:, 0:nh, :], dv[:, 0:nh, :], AF.Square, [dv], [SQ])
            self.P.add("dve", (lambda o, i_: (lambda e: e.tensor_reduce(out=o, in_=i_, axis=AX.X, op=ALU.add)))(
                s8[:, 2, 0:nh], SQ[:, 0:nh, :]), reads=[SQ], writes=[s8])
            self.powneg(s8[:, 3, 0:nh], s8[:, 2, 0:nh], [s8], s8, 0.5, scale=1.0 / 64, bias=GN_EPS)
            self.tt("dve", dv[:, 0:nh, :], dv[:, 0:nh, :], s8[:, 3, 0:nh].unsqueeze(2).broadcast_to([128, nh, 64]),
                    ALU.mult, [dv, s8], [dv])
            for fc in range(4):
                bk = self.bank()
                for c in range(n):
                    self.tr(bk, bk[:, c * 128:(c + 1) * 128],
                            dv[:, c * 8 + 2 * fc:c * 8 + 2 * fc + 2, :].rearrange("p a b -> p (a b)"), ident.ap, [dv, ident])
                self.act(rw1[:, fc, 0:n * 128], bk[:, 0:n * 128], AF.Identity, [bk, lnT], [(rw1, fc)],
                         scale=lnT[:, 0, fc:fc + 1], bias=lnT[:, 1, fc:fc + 1])
            self.tt("dve", rw1[:, :, 0:n * 128], rw1[:, :, 0:n * 128], bvl[:, :, 0:n * 128], ALU.add, [rw1, bvl], [rw1])
            self.tt("pool", rwo[:, :, 0:n * 128], rw1[:, :, 0:n * 128], gl[:, :, 0:n * 128], ALU.mult, [rw1, gl], [rwo])
            self.dma("sp", S["rwT"][:, :, tokc].rearrange("c p t -> p c t"), rwo[:, :, 0:n * 128], [rwo], [B["rwT"]])
        self.P.barrier()
        self.off = self.markP

    def phaseC1(self):
        I, B, S = self.inp, self.dbufs, self.scr
        A = self.alloc
        wpa = A("wpa", [64, 8, D], BF16)
        self.dma("pool", wpa.ap, I["w_pa"], [B["w_pa"]], [wpa])
        wpb = A("wpb", [128, 4, D], BF16)
        self.dma("pool", wpb.ap, I["w_pb"], [B["w_pb"]], [wpb])
        wo = A("wo", [128, 8, D], BF16)
        self.dma("pool", wo.ap, I["w_out"], [B["w_out"]], [wo])
        at = [A("at%d" % i, [64, 8, 512], BF16) for i in range(2)]
        rw = [A("rw%d" % i, [128, 4, 512], BF16) for i in range(2)]
        sg = [A("sg%d" % i, [128, 16, 512], BF16) for i in range(2)]
        xg = [A("xc%d" % i, [128, 8, 512], F32) for i in range(2)]
        mg = A("mg", [128, 8, 512], BF16)
        x1 = A("x1", [128, 8, 512], F32)
        t1 = [A("c1t%d" % i, [128, 512], F32) for i in range(2)]
        t2 = [A("c1u%d" % i, [128, 512], F32) for i in range(2)]
        xTv = I["xT"].rearrange("(k p) t -> p k t", p=128)

        def loads(g):
            tok = slice(g * 512, (g + 1) * 512)
            self.dma("sp", at[g % 2].ap, S["attnT"][:, :, tok], [B["attnT"]], [at[g % 2]])
            self.dma("sp", rw[g % 2].ap, S["rwT"][:, :, tok].rearrange("c p t -> p c t"), [B["rwT"]], [rw[g % 2]])
            self.dma("sp", sg[g % 2].ap, S["sgT"][:, :, tok].rearrange("c p t -> p c t"), [B["sgT"]], [sg[g % 2]])
            self.dma("sp", xg[g % 2].ap, xTv[:, :, tok], [B["xT"]], [xg[g % 2]])

        loads(0)
        for g in range(6):
            w = 0 if g < 4 else 1
            tok = slice(g * 512, (g + 1) * 512)
            if g < 5:
                loads(g + 1)
            a_, r_, s_, x_ = at[g % 2], rw[g % 2], sg[g % 2], xg[g % 2]
            for n in range(8):
                ns = slice(n * 128, (n + 1) * 128)
                ba = self.bank()
                for h in range(8):
                    self.mm(ba, ba.ap, wpa[:, h, ns], a_[:, h, :], [wpa, a_], start=(h == 0), stop=(h == 7))
                bb = self.bank()
                for fc in range(4):
                    self.mm(bb, bb.ap, wpb[:, fc, ns], r_[:, fc, :], [wpb, r_], start=(fc == 0), stop=(fc == 3))
                u1, u2 = t1[n % 2], t2[n % 2]
                self.tt("dve", u1.ap, ba.ap, s_[:, n, :], ALU.mult, [ba, s_], [u1])
                self.tt("dve", u2.ap, bb.ap, s_[:, 8 + n, :], ALU.mult, [bb, s_], [u2])
                self.tt("pool", mg[:, n, :], u1.ap, u2.ap, ALU.add, [u1, u2], [(mg, n)])
            for n in range(8):
                ns = slice(n * 128, (n + 1) * 128)
                bo = self.bank()
                for k in range(8):
                    self.mm(bo, bo.ap, wo[:, k, ns], mg[:, k, :], [wo, (mg, k)], start=(k == 0), stop=(k == 7))
                self.stt("dve", x1[:, n, :], bo.ap, self.msc(2, n, w), x_[:, n, :], ALU.mult, ALU.add,
                         [bo, self.MS, x_], [(x1, n)])
            self.dma("sp", S["x1T"][:, :, tok].rearrange("c p t -> p c t"), x1.ap, [x1], [B["x1T"]])
        self.P.barrier()
        self.off = self.markP

    def phaseC2(self):
        I, B, S = self.inp, self.dbufs, self.scr
        A = self.alloc
        wup = A("wup", [128, 8, 2 * DFF], BF16)
        fbounds = [0, 4, 10, 16, NFC]
        for bi_ in range(4):
            for half_ in range(2):
                c0_ = half_ * DFF + fbounds[bi_] * 128
                c1_ = half_ * DFF + fbounds[bi_ + 1] * 128
                self.dma("pool", wup[:, :, c0_:c1_], I["w_up"][:, :, c0_:c1_], [B["w_up"]], [(wup, bi_)])

        def fblk(fc_):
            return max(i for i in range(4) if fbounds[i] <= fc_)
        wdn = A("wdn", [128, NFC, D], BF16)
        for q in range(2):
            self.dma("pool", wdn[:, q * 11:(q + 1) * 11, :], I["w_dn"][:, q * 11:(q + 1) * 11, :], [B["w_dn"]], [(wdn, q)])
        cw = A("cw", [128, 2 * NFC, 3], F32)
        self.dma("sp", cw.ap, I["convw"], [B["convw"]], [cw])
        cb = A("cb", [128, 2 * NFC], F32)
        self.dma("sp", cb.ap, I["convb"], [B["convb"]], [cb])
        X1 = [A("X1_%d" % i, [128, 8, 258], F32) for i in range(2)]
        h2 = A("h2", [128, 8, 258], BF16)
        aT = A("aT", [128, NFC, 256], BF16)
        x2 = A("x2", [128, 8, 256], F32)
        yst = A("yst", [128, 8, 256], F32)
        rstd = A("rstd2", [128, 258], F32)
        sq = [A("sq2_%d" % i, [128, 258], BF16) for i in range(2)]
        tx = [A("tx2_%d" % i, [128, 258], F32) for i in range(2)]
        cv = [A("cv%d" % i, [128, 256], F32) for i in range(2)]
        cg_ = [A("cg%d" % i, [128, 256], F32) for i in range(2)]
        sgl = [A("sgl%d" % i, [128, 256], F32) for i in range(2)]
        ones, GS = self.onesb, self.GS
        seqs = [(0, 2048)] + [(2048 + 256 * i, 256) for i in range(4)]
        blocks = []
        for (s0, Tn) in seqs:
            for b in range(Tn // 256):
                blocks.append((s0, Tn, b))

        def load(bi):
            s0, Tn, b = blocks[bi]
            t0 = s0 + 256 * b
            clo = 0 if b > 0 else 1
            chi = 258 if b < Tn // 256 - 1 else 257
            X = X1[bi % 2]
            self.dma("sp", X[:, :, clo:chi], S["x1T"][:, :, t0 - 1 + clo:t0 - 1 + chi].rearrange("c p t -> p c t"),
                     [B["x1T"]], [X])

        h2s = [h2, A("h2b", [128, 8, 258], BF16)]

        def geom(bi):
            s0, Tn, b = blocks[bi]
            clo = 0 if b > 0 else 1
            chi = 258 if b < Tn // 256 - 1 else 257
            return s0, Tn, b, (0 if s0 == 0 else 1), s0 + 256 * b, clo, chi

        def front(bi):
            s0, Tn, b, w, t0, clo, chi = geom(bi)
            wd = chi - clo
            cs = slice(clo, chi)
            X = X1[bi % 2]
            hh = h2s[bi % 2]
            bk = self.bank()
            for k in range(8):
                s_ = sq[k % 2]
                self.act(s_[:, cs], X[:, k, cs], AF.Square, [X], [s_])
                self.mm(bk, bk[:, 0:wd], ones.ap, s_[:, cs], [ones, s_], start=(k == 0), stop=(k == 7))
            self.powneg(rstd[:, cs], bk[:, 0:wd], [bk], rstd, 0.5, scale=1.0 / D, bias=NORM_EPS)
            for k in range(8):
                t_ = tx[k % 2]
                self.tt("dve", t_[:, cs], X[:, k, cs], rstd[:, cs], ALU.mult, [X, rstd], [t_])
                self.act(hh[:, k, cs], t_[:, cs], AF.Identity, [t_, GS, self.MS], [(hh, k)],
                         scale=GS[:, 1, k, w:w + 1], bias=self.msc(3, k, w))

        def upconv(bi):
            s0, Tn, b, w, t0, clo, chi = geom(bi)
            wd = chi - clo
            cs = slice(clo, chi)
            hh = h2s[bi % 2]
            o = 1 - clo
            for fc in range(NFC):
                res = []
                for half in range(2):
                    col = half * DFF + fc * 128
                    ch = half * NFC + fc
                    bu = self.bank()
                    for k in range(8):
                        self.mm(bu, bu[:, 0:wd], wup[:, k, col:col + 128], hh[:, k, cs], [(wup, fblk(fc)), (hh, k)],
                                start=(k == 0), stop=(k == 7))
                    dst = (cv if half == 0 else cg_)[fc % 2]
                    self.act(dst.ap, bu[:, o:o + 256], AF.Identity, [bu, cw, cb], [dst],
                             scale=cw[:, ch, 1:2], bias=cb[:, ch:ch + 1])
                    j0 = 1 if clo == 1 else 0
                    self.stt("dve", dst[:, j0:256], bu[:, o + j0 - 1:o + 255], cw[:, ch, 0:1], dst[:, j0:256],
                             ALU.mult, ALU.add, [bu, cw, dst], [dst])
                    j1 = 255 if chi == 257 else 256
                    self.stt("dve", dst[:, 0:j1], bu[:, o + 1:o + 1 + j1], cw[:, ch, 2:3], dst[:, 0:j1],
                             ALU.mult, ALU.add, [bu, cw, dst], [dst])
                    res.append(dst)
                sl = sgl[fc % 2]
                self.act(sl.ap, res[1].ap, AF.Silu, [res[1]], [sl])
                self.tt("pool", aT[:, fc, :], sl.ap, res[0].ap, ALU.mult, [sl, res[0]], [(aT, fc)])

        def downfinal(bi):
            s0, Tn, b, w, t0, clo, chi = geom(bi)
            X = X1[bi % 2]
            for n in range(8):
                ns = slice(n * 128, (n + 1) * 128)
                bd = self.bank()
                for fc in range(NFC):
                    self.mm(bd, bd[:, 0:256], wdn[:, fc, ns], aT[:, fc, :], [(wdn, fc // 11), (aT, fc)],
                            start=(fc == 0), stop=(fc == NFC - 1))
                self.stt("dve", x2[:, n, :], bd[:, 0:256], self.msc(5, n, w), X[:, n, 1:257], ALU.mult, ALU.add,
                         [bd, self.MS, X], [(x2, n)])
            bk = self.bank()
            for n in range(8):
                s_ = sq[n % 2]
                self.act(s_[:, 0:256], x2[:, n, :], AF.Square, [(x2, n)], [s_])
                self.mm(bk, bk[:, 0:256], ones.ap, s_[:, 0:256], [ones, s_], start=(n == 0), stop=(n == 7))
            self.powneg(rstdf.ap, bk[:, 0:256], [bk], rstdf, 0.5, scale=1.0 / D, bias=NORM_EPS)
            for n in range(8):
                self.stt("dve", yst[:, n, :], x2[:, n, :], self.gT[:, 2, n:n + 1],
                         rstdf.ap, ALU.mult, ALU.mult, [(x2, n), self.gT, rstdf], [(yst, n)])
            self.dma("sp", self.out["yT"].rearrange("(k p) t -> p k t", p=128)[:, :, t0:t0 + 256], yst.ap,
                     [yst], [B["yT"]])

        rstdf = A("rstdf", [128, 256], F32)
        load(0)
        load(1)
        front(0)
        for bi in range(len(blocks)):
            upconv(bi)
            if bi + 1 < len(blocks):
                front(bi + 1)
            downfinal(bi)
            if bi + 2 < len(blocks):
                load(bi + 2)
        self.P.barrier()


def _fm(v, nchunk):
    return np.ascontiguousarray(np.asarray(v, np.float32).reshape(nchunk, 128).T)


def _consts():
    idx = np.arange(128)
    p = idx[:, None]
    f = idx[None, :]
    masks = np.stack([(f < p), (f > p), (f <= p), (f >= p)], axis=1).astype(np.float32)
    ident = np.eye(128, dtype=np.float32)
    blockones = np.kron(np.eye(2, dtype=np.float32), np.ones((64, 64), np.float32))
    sel = np.zeros((128, 2, 128), np.float32)
    for dd in range(2):
        sel[idx, dd, dd * 64 + idx % 64] = 1.0
    T = NS
    row = np.repeat(np.arange(T // 64), 64).astype(np.float32)
    col = np.tile(np.arange(64), T // 64).astype(np.float32)
    freqs = (np.float32(10000.0) ** (-np.arange(16, dtype=np.float32) / np.float32(16))).astype(np.float32)
    rope = np.zeros((128, 2, T), np.float32)
    permT = np.zeros((128, 128), np.float32)
    for pp in range(128):
        dd = pp % 64
        pos = row if dd < 32 else col
        ang = (pos * freqs[dd % 16]).astype(np.float32)
        first = (dd % 32) < 16
        rope[pp, 0] = np.cos(ang)
        rope[pp, 1] = (-np.sin(ang)) if first else np.sin(ang)
        partner = pp + 16 if first else pp - 16
        permT[partner, pp] = 1.0
    return dict(ident=ident, masks=masks, blockones=blockones, sel=sel, rope=rope, permT=permT)


def _prep_shared(inp):
    f = lambda a: np.ascontiguousarray(np.asarray(a, np.float32))
    sh = {}
    sh["w_ada"] = f(inp["w_ada"][0])
    sh["b_adaT"] = _fm(inp["b_ada"][0], 48)
    sh["gT"] = np.ascontiguousarray(np.stack([_fm(inp["g_norm1"][0], 8), _fm(inp["g_norm2"][0], 8),
                                              _fm(inp["g_final"], 8)], axis=1))
    w_in = f(inp["w_in"][0])
    qcols = []
    for a in range(4):
        qcols += list(range(a * 64, a * 64 + 64)) + list(range((4 + a) * 64, (4 + a) * 64 + 64))
    cols = qcols + list(range(512, WCOLS))
    sh["w_in"] = np.ascontiguousarray(w_in[:, cols])
    sh["sink"] = f(inp["attn_sink"][0]).reshape(1, 8)
    sh["w_pa"] = np.ascontiguousarray(f(inp["w_proj_a"][0]).reshape(8, 64, D).transpose(1, 0, 2))
    sh["w_pb"] = np.ascontiguousarray(f(inp["w_proj_b"][0]).reshape(4, 128, D).transpose(1, 0, 2))
    sh["w_out"] = np.ascontiguousarray(f(inp["w_out"][0]).reshape(8, 128, D).transpose(1, 0, 2))
    sh["w_up"] = np.ascontiguousarray(f(inp["w_ffn_up"][0]).reshape(8, 128, 2 * DFF).transpose(1, 0, 2))
    sh["w_dn"] = np.ascontiguousarray(f(inp["w_ffn_down"][0]).reshape(NFC, 128, D).transpose(1, 0, 2))
    cw = f(inp["ffn_conv_w"][0])
    sh["convw"] = np.ascontiguousarray(cw.T.reshape(2 * NFC, 128, 3).transpose(1, 0, 2))
    sh["convb"] = _fm(inp["ffn_conv_b"][0], 2 * NFC)
    mu = f(inp["rwkv_mu"][0])
    sh["mu"] = np.ascontiguousarray(mu.T.reshape(15, 128, 2).transpose(1, 0, 2))
    w0 = f(inp["rwkv_w0"][0])
    a0 = f(inp["rwkv_a0"][0])
    w0a0 = np.stack([w0, a0], axis=0).reshape(2, 2, 4, 128)
    sh["w0a0"] = np.ascontiguousarray(w0a0.transpose(3, 0, 1, 2))
    sh["w2"] = f(inp["rwkv_w2"][0]).reshape(128, 512)
    sh["a2"] = f(inp["rwkv_a2"][0]).reshape(128, 512)
    sh["g2"] = f(inp["rwkv_g2"][0])
    sh["kvec"] = np.ascontiguousarray(np.stack([_fm(inp["rwkv_k_k"][0], 4), _fm(inp["rwkv_k_a"][0], 4),
                                                _fm(inp["rwkv_r_k"][0].reshape(-1), 4)], axis=1))
    sh["lnT"] = np.ascontiguousarray(np.stack([_fm(inp["rwkv_ln_g"][0], 4), _fm(inp["rwkv_ln_b"][0], 4)], axis=1))
    sh.update(_consts())
    return sh


def _prep_core(inp, c):
    f = lambda a: np.ascontiguousarray(np.asarray(a, np.float32))
    m = {}
    xs = f(inp["x_sample"][c])
    xp = f(inp["x_prompt"][4 * c:4 * c + 4]).reshape(1024, D)
    m["xT"] = np.ascontiguousarray(np.concatenate([xs, xp], axis=0).T)
    cv = np.stack([_fm(inp["c"][c], 8), _fm(inp["c_ctx"], 8)], axis=2)
    m["cvec"] = np.ascontiguousarray(cv)
    ck = f(inp["cache_k"][c, 0]).reshape(512, 128)
    m["kcT"] = np.ascontiguousarray(ck.T)
    cvv = f(inp["cache_v"][c, 0]).reshape(4, 128, 128)
    m["vc"] = np.ascontiguousarray(cvv.transpose(1, 0, 2))
    st = f(inp["state_rwkv"][c, 0])
    m["st0"] = np.ascontiguousarray(st.transpose(0, 3, 1, 2).reshape(128, 512))
    return m


_CACHE = {}


def _get_nc(debug=False, upto="C2"):
    key = (debug, upto)
    if key not in _CACHE:
        _CACHE[key] = KB(debug=debug, upto=upto)
        _CACHE[key].build()
    return _CACHE[key]


def kernel(**inputs):
    kb = _get_nc()
    sh = _prep_shared(inputs)
    in_maps = []
    for c in range(NCORES):
        m = dict(sh)
        m.update(_prep_core(inputs, c))
        in_maps.append(m)
    res = run_bass_kernel_spmd(kb.nc, in_maps, core_ids=list(range(NCORES)))
    y_prompt = np.zeros((32, 256, D), np.float32)
    y_sample = np.zeros((8, 2048, D), np.float32)
    new_k = np.zeros((32, 1, 256, 2, 64), np.float32)
    new_v = np.zeros((32, 1, 256, 2, 64), np.float32)
    new_s = np.zeros((32, 1, 2, 8, 64, 64), np.float32)
    for c in range(NCORES):
        r = res.results[c]
        y = np.asarray(r["yT"]).T
        y_sample[c] = y[:2048]
        y_prompt[4 * c:4 * c + 4] = y[2048:].reshape(4, 256, D)
        new_k[4 * c:4 * c + 4, 0] = np.asarray(r["newk"]).reshape(4, 256, 2, 64)
        new_v[4 * c:4 * c + 4, 0] = np.asarray(r["newv"]).reshape(4, 256, 2, 64)
        st = np.asarray(r["newst"]).reshape(4, 2, 64, 8, 64)
        new_s[4 * c:4 * c + 4, 0] = st.transpose(0, 1, 3, 4, 2)
    return (y_prompt, y_sample, new_k, new_v, new_s)
```

```python
import contextlib
import numpy as np
import concourse.bass as bass
import concourse.mybir as mybir
from concourse.bass_utils import run_bass_kernel_spmd

F32 = mybir.dt.float32
BF16 = mybir.dt.bfloat16
U8 = mybir.dt.uint8
AF = mybir.ActivationFunctionType
ALU = mybir.AluOpType
AX = mybir.AxisListType

ENGS = ("pe", "act", "dve", "pool", "sp")
NCORES = 8
D = 1024
NT = 3072
NS = 2048
WCOLS = 4736
DFF = 2816
NFC = 22
CDEC = -0.6065306597126334
NORM_EPS = 1e-6
GN_EPS = 64e-5


class Buf:
    __slots__ = ("name", "st", "excl")

    def __init__(self, name, excl=False):
        self.name = name
        self.excl = excl
        self.st = {}

    def _conf(self, part):
        if part is None:
            return list(self.st.values())
        out = []
        if part in self.st:
            out.append(self.st[part])
        if None in self.st:
            out.append(self.st[None])
        return out

    def on_read(self, op, part, deps):
        for s in self._conf(part):
            if s[0] is not None:
                deps.add(s[0])
            if self.excl:
                for r in s[1]:
                    if r.eng != op.eng:
                        deps.add(r)
        self.st.setdefault(part, [None, []])[1].append(op)

    def on_write(self, op, part, deps):
        for s in self._conf(part):
            if s[0] is not None:
                deps.add(s[0])
            deps.update(s[1])
        if part is None:
            self.st = {None: [op, []]}
        else:
            self.st[part] = [op, []]


class Op:
    __slots__ = ("eng", "fn", "deps", "dma", "sem", "val", "signals", "prewait", "idx", "hard")

    def __init__(self, eng, fn, dma):
        self.eng = eng
        self.fn = fn
        self.hard = ()
        self.deps = set()
        self.dma = dma
        self.sem = None
        self.val = 0
        self.signals = dma
        self.prewait = None
        self.idx = 0


class Prog:
    NPOOL = 8

    def __init__(self, nc):
        self.nc = nc
        self.ops = []
        self.last = {e: None for e in ENGS}
        self.dma_since_barrier = []

    @staticmethod
    def _norm(lst):
        out = []
        for x in lst:
            if x is None:
                continue
            if isinstance(x, tuple):
                out.append((x[0].buf if hasattr(x[0], "buf") else x[0], x[1]))
            else:
                out.append((x.buf if hasattr(x, "buf") else x, None))
        return out

    def add(self, eng, fn, reads=(), writes=(), dma=False, hard=()):
        op = Op(eng, fn, dma)
        op.hard = tuple(hard)
        op.deps.update(op.hard)
        op.idx = len(self.ops)
        for b, p in self._norm(reads):
            b.on_read(op, p, op.deps)
        for b, p in self._norm(writes):
            b.on_write(op, p, op.deps)
        op.deps.discard(op)
        self.ops.append(op)
        self.last[eng] = op
        if dma:
            self.dma_since_barrier.append(op)
        return op

    def barrier(self):
        lasts = [o for o in self.last.values() if o is not None]
        dmas = list(self.dma_since_barrier)
        self.dma_since_barrier = []
        for e in ENGS:
            op = Op(e, None, False)
            op.idx = len(self.ops)
            op.deps = set(lasts) | set(dmas)
            self.ops.append(op)
            self.last[e] = op

    def emit(self):
        nc = self.nc
        for op in self.ops:
            per_eng = {}
            nd = set()
            for d in op.deps:
                if d.fn is None:
                    continue
                if d.dma:
                    nd.add(d)
                    continue
                if d.eng == "pe" and op.eng == "pe" and not op.dma and d not in op.hard:
                    continue
                c = per_eng.get(d.eng)
                if c is None or d.idx > c.idx:
                    per_eng[d.eng] = d
            nd.update(per_eng.values())
            op.deps = nd
            for d in nd:
                d.signals = True
        with contextlib.ExitStack() as es:
            esem = {e: es.enter_context(nc.semaphore("sem_" + e)) for e in ENGS}
            pools = {e: [es.enter_context(nc.semaphore("dq_%s_%d" % (e, i))) for i in range(self.NPOOL)]
                     for e in ("sp", "act", "pool")}
            pool_tot = {e: [0] * self.NPOOL for e in pools}
            pool_rr = {e: 0 for e in pools}
            cnt = {e: 0 for e in ENGS}
            for op in self.ops:
                if op.fn is None:
                    continue
                if op.dma:
                    q = op.eng
                    i = pool_rr[q]
                    pool_rr[q] = (i + 1) % self.NPOOL
                    op.prewait = (pools[q][i], pool_tot[q][i])
                    pool_tot[q][i] += 16
                    op.sem = pools[q][i]
                    op.val = pool_tot[q][i]
                elif op.signals:
                    cnt[op.eng] += 1
                    op.sem = esem[op.eng]
                    op.val = cnt[op.eng]
            by_eng = {e: [o for o in self.ops if o.eng == e] for e in ENGS}
            self.stats = {e + "_ops": len([o for o in by_eng[e] if o.fn is not None]) for e in ENGS}
            self.stats.update({e + "_sig": cnt[e] for e in ENGS})
            final = []
            for q in pools:
                for i in range(self.NPOOL):
                    if pool_tot[q][i] > 0:
                        final.append((pools[q][i], pool_tot[q][i]))
            blk = es.enter_context(nc.Block())

            def run(e, ename):
                waited = {}

                def w(sem, val):
                    if sem is None or val <= 0:
                        return
                    k = id(sem)
                    if waited.get(k, 0) >= val:
                        return
                    waited[k] = val
                    self.stats[ename + "_waits"] = self.stats.get(ename + "_waits", 0) + 1
                    e.wait_ge(sem, val)

                for op in by_eng[ename]:
                    for d in sorted(op.deps, key=lambda o: o.idx):
                        w(d.sem, d.val)
                    if op.fn is None:
                        continue
                    if op.dma:
                        w(*op.prewait)
                        op.fn(e).then_inc(op.sem, 16)
                    else:
                        ins = op.fn(e)
                        if op.signals:
                            ins.then_inc(op.sem, 1)
                if ename == "sp":
                    for s, v in final:
                        w(s, v)
                    for en in ENGS:
                        if cnt[en] > 0:
                            w(esem[en], cnt[en])

            @blk.tensor
            def _(e):
                run(e, "pe")

            @blk.scalar
            def _(e):
                run(e, "act")

            @blk.vector
            def _(e):
                run(e, "dve")

            @blk.gpsimd
            def _(e):
                run(e, "pool")

            @blk.sync
            def _(e):
                run(e, "sp")
        return cnt


class T:
    def __init__(self, ap, name, excl=False):
        self.ap = ap
        self.buf = Buf(name, excl)

    def __getitem__(self, k):
        return self.ap[k]


ARENA = 204 * 1024


class KB:
    def __init__(self, debug=False, upto="C2"):
        self.debug = debug
        self.upto = upto
        self.nc = bass.Bass("TRN2", target_bir_lowering=False)
        self.P = Prog(self.nc)
        self.inp = {}
        self.out = {}
        self.scr = {}
        self.dbufs = {}
        self._rr = 0
        self._alt = 0

    def din(self, name, shape, dt=F32):
        self.inp[name] = self.nc.dram_tensor(name, list(shape), dt, kind="ExternalInput").ap()
        self.dbufs[name] = Buf("d_" + name)
        return self.inp[name]

    def dout(self, name, shape, dt=F32):
        self.out[name] = self.nc.dram_tensor(name, list(shape), dt, kind="ExternalOutput").ap()
        self.dbufs[name] = Buf("d_" + name)
        return self.out[name]

    def dscr(self, name, shape, dt=F32):
        kind = "ExternalOutput" if self.debug else "Internal"
        self.scr[name] = self.nc.dram_tensor(name, list(shape), dt, kind=kind).ap()
        self.dbufs[name] = Buf("d_" + name)
        return self.scr[name]

    def alloc(self, name, shape, dt, top=False):
        esz = 4 if dt == F32 else 2
        n = int(np.prod(shape[1:])) * esz
        assert self.off + n <= self.top, ("SBUF overflow", name, self.off, n, self.top)
        if top:
            self.top -= (n + 63) // 64 * 64
            ap = self.arena[:, self.top:self.top + n].bitcast(dt)
        else:
            ap = self.arena[:, self.off:self.off + n].bitcast(dt)
            self.off += (n + 63) // 64 * 64
        if len(shape) == 3:
            ap = ap.rearrange("p (a b) -> p a b", a=shape[1])
        elif len(shape) == 4:
            ap = ap.rearrange("p (a b c) -> p a b c", a=shape[1], b=shape[2])
        if shape[0] < 128:
            ap = ap[0:shape[0]]
        return T(ap, name)

    def bank(self):
        b = self.ps[self._rr]
        self._rr = (self._rr + 1) % 8
        return b

    def alt(self, engs=("act", "dve")):
        self._alt += 1
        return engs[self._alt % len(engs)]

    def dma(self, eng, out, in_, reads, writes):
        self.P.add(eng, lambda e: e.dma_start(out=out, in_=in_), reads=reads, writes=writes, dma=True)

    def mm(self, bank, out, lhsT, rhs, reads, start=True, stop=True, hard=()):
        return self.P.add("pe", lambda e: e.matmul(out, lhsT=lhsT, rhs=rhs, start=start, stop=stop),
                          reads=reads, writes=[bank], hard=hard)

    def mm4(self, bank, hd, fn):
        prev = None
        for u in (0, 2, 1, 3):
            prev = fn(u, (prev,) if u == 1 else ())
        return prev

    def tr(self, bank, out, in_, ident, reads):
        self.P.add("pe", lambda e: e.transpose(out=out, in_=in_, identity=ident), reads=reads, writes=[bank])

    def act(self, out, in_, func, reads, writes, scale=1.0, bias=0.0, eng="act"):
        self.P.add("act", lambda e: e.activation(out=out, in_=in_, func=func, bias=bias, scale=scale),
                   reads=reads, writes=writes)

    def tt(self, eng, out, in0, in1, op, reads, writes):
        self.P.add(eng, lambda e: e.tensor_tensor(out=out, in0=in0, in1=in1, op=op), reads=reads, writes=writes)

    def ts(self, eng, out, in0, s1, s2, op0, op1, reads, writes):
        if s2 is None:
            self.P.add(eng, lambda e: e.tensor_scalar(out=out, in0=in0, scalar1=s1, scalar2=None, op0=op0),
                       reads=reads, writes=writes)
        else:
            self.P.add(eng, lambda e: e.tensor_scalar(out=out, in0=in0, scalar1=s1, scalar2=s2, op0=op0, op1=op1),
                       reads=reads, writes=writes)

    def stt(self, eng, out, in0, scalar, in1, op0, op1, reads, writes):
        eng = "dve"
        self.P.add(eng, lambda e: e.scalar_tensor_tensor(out=out, in0=in0, scalar=scalar, in1=in1, op0=op0, op1=op1),
                   reads=reads, writes=writes)

    def copy(self, eng, out, in_, reads, writes):
        if eng == "act":
            self.P.add("act", lambda e: e.activation(out=out, in_=in_, func=AF.Copy), reads=reads, writes=writes)
        else:
            self.P.add(eng, lambda e: e.tensor_copy(out=out, in_=in_), reads=reads, writes=writes)

    def memset(self, eng, out, val, writes):
        self.P.add(eng, lambda e: e.memset(out, val), writes=writes)

    def recip(self, out, in_, reads, writes):
        self.P.add("dve", lambda e: e.reciprocal(out=out, in_=in_), reads=reads, writes=writes)

    def powneg(self, out, in_, reads, wt, expo, scale=1.0, bias=0.0):
        self.act(out, in_, AF.Ln, reads, [wt], scale=scale, bias=bias)
        self.act(out, out, AF.Exp, [wt], [wt], scale=-expo)

    def declare(self):
        di = self.din
        di("xT", [D, NT])
        di("cvec", [128, 8, 2])
        di("kcT", [128, 512])
        di("vc", [128, 4, 128])
        di("st0", [128, 512])
        di("w_ada", [D, 6 * D])
        di("b_adaT", [128, 48])
        di("gT", [128, 3, 8])
        di("w_in", [D, WCOLS])
        di("sink", [1, 8])
        di("w_pa", [64, 8, D])
        di("w_pb", [128, 4, D])
        di("w_out", [128, 8, D])
        di("w_up", [128, 8, 2 * DFF])
        di("w_dn", [128, NFC, D])
        di("convw", [128, 2 * NFC, 3])
        di("convb", [128, 2 * NFC])
        di("mu", [128, 15, 2])
        di("w0a0", [128, 2, 2, 4])
        di("w2", [128, 512])
        di("a2", [128, 512])
        di("g2", [128, 512])
        di("kvec", [128, 3, 4])
        di("lnT", [128, 2, 4])
        di("ident", [128, 128])
        di("masks", [128, 4, 128])
        di("blockones", [128, 128])
        di("sel", [128, 2, 128])
        di("rope", [128, 2, NS])
        di("permT", [128, 128])
        do = self.dout
        do("yT", [D, NT])
        do("newk", [1024, 128])
        do("newv", [1024, 128])
        do("newst", [4, 128, 512])
        ds = self.dscr
        ds("zrT", [15, 128, NT])
        ds("sgT", [16, 128, NT], BF16)
        ds("attnT", [64, 8, NT], BF16)
        ds("rwT", [4, 128, NT], BF16)
        ds("gTs", [4, 128, NT])
        ds("bvT", [4, 128, NT])
        ds("chG", [24, 128, 8, 128], BF16)
        ds("chH", [24, 128, 8, 64], BF16)
        ds("chZ", [24, 128, 8, 64])
        ds("y0", [24, 128, 512])
        ds("wcD", [24, 128, 2, 4])
        ds("x1T", [8, 128, NT])
        if self.debug:
            ds("dbg_mod", [128, 96])
            ds("dbg_q", [128, 4, NT], BF16)
            ds("dbg_k", [128, NT], BF16)
            ds("dbg_v", [128, 24 * 130], BF16)
            ds("dbg_y", [24, 128, 512])

    def build(self):
        nc = self.nc
        self.declare()
        with contextlib.ExitStack() as es:
            self.arena = es.enter_context(nc.sbuf_tensor("arena", [128, ARENA], U8))
            self.off = 0
            self.top = ARENA
            self.ps = []
            for i in range(8):
                t = es.enter_context(nc.psum_tensor("ps%d" % i, [128, 512], F32))
                self.ps.append(T(t[:, :], "ps%d" % i, excl=True))
            self.phase0()
            if self.upto != "0":
                self.phaseA()
            if self.upto not in ("0", "A"):
                self.phaseB1()
            if self.upto not in ("0", "A", "B1"):
                self.phaseB2()
            if self.upto not in ("0", "A", "B1", "B2"):
                self.phaseC1()
            if self.upto not in ("0", "A", "B1", "B2", "C1"):
                self.phaseC2()
            self.P.emit()
        return nc

    def phase0(self):
        I, B = self.inp, self.dbufs
        A = self.alloc

        def load(name, shape, src=None, dt=F32, eng="sp"):
            t = A(name, shape, dt)
            s = I[name] if src is None else src
            self.dma(eng, t.ap, s, [B[name]], [t])
            return t

        self.ident = load("ident", [128, 128])
        self.masks = load("masks", [128, 4, 128])
        self.bones = load("blockones", [128, 128])
        self.sel = load("sel", [128, 2, 128])
        self.gT = load("gT", [128, 3, 8])
        self.ones = A("ones", [128, 128], F32)
        self.memset("pool", self.ones.ap, 1.0, [self.ones])
        self.onesb = A("onesb", [128, 128], BF16)
        self.memset("pool", self.onesb.ap, 1.0, [self.onesb])
        self.identb = A("identb", [128, 128], BF16)
        self.copy("dve", self.identb.ap, self.ident.ap, [self.ident], [self.identb])
        self.esink = A("esink", [128, 8], F32)
        self.dma("sp", self.esink[64:65, :], I["sink"], [B["sink"]], [self.esink])
        self.act(self.esink[64:65, :], self.esink[64:65, :], AF.Exp, [self.esink], [self.esink])
        self.MS = A("MS", [128, 48, 2], F32)
        self.GS = A("GS", [128, 2, 8, 2], F32)
        mark = self.off
        self.markP = mark
        self.W = A("W", [128, 8, WCOLS], BF16, top=True)
        self.wbounds = [0, 768, 1792, 2816, 3840, WCOLS]
        w_in_v = I["w_in"].rearrange("(k p) c -> p k c", p=128)
        for bi_ in range(5):
            c0_, c1_ = self.wbounds[bi_], self.wbounds[bi_ + 1]
            self.dma("pool", self.W[:, :, c0_:c1_], w_in_v[:, :, c0_:c1_], [B["w_in"]], [(self.W, bi_)])
        cv = load("cvec", [128, 8, 2])
        bT = load("b_adaT", [128, 48])
        sc = A("sc", [128, 8, 2], F32)
        self.act(sc.ap, cv.ap, AF.Silu, [cv], [sc])
        accr = A("accr", [2, 6 * D], F32)
        self.memset("pool", accr.ap, 0.0, [accr])
        wa = [A("wa%d" % i, [128, 6 * D], F32) for i in range(3)]
        for k in range(8):
            buf = wa[k % 3]
            self.dma("sp", buf.ap, I["w_ada"][k * 128:(k + 1) * 128, :], [B["w_ada"]], [buf])
            for n in range(12):
                bk = self.bank()
                self.mm(bk, bk[0:2, :], sc[:, k, :], buf[:, n * 512:(n + 1) * 512], [buf, sc])
                self.tt("dve", accr[:, n * 512:(n + 1) * 512], accr[:, n * 512:(n + 1) * 512], bk[0:2, :], ALU.add,
                        [(accr, n), bk], [(accr, n)])
        acc = A("acc", [128, 96], F32)
        for q in range(2):
            bk = self.bank()
            for j in range(24):
                jj = q * 24 + j
                self.tr(bk, bk[:, 2 * j:2 * j + 2], accr[:, jj * 128:(jj + 1) * 128], self.ident[0:2, 0:2],
                        [(accr, jj // 4), self.ident])
            self.copy("act", acc[:, q * 48:(q + 1) * 48], bk[:, 0:48], [bk], [acc])
        self.tt("dve", self.MS.ap, acc.ap.rearrange("p (a b) -> p a b", b=2),
                bT.ap.unsqueeze(2).broadcast_to([128, 48, 2]), ALU.add, [acc, bT], [self.MS])
        for ni, (sv, gi) in enumerate(((1, 0), (4, 1))):
            tmp = A("gstmp%d" % ni, [128, 8, 2], F32)
            self.ts("dve", tmp.ap, self.MS[:, sv * 8:(sv + 1) * 8, :], 1.0, None, ALU.add, None, [self.MS], [tmp])
            self.tt("dve", self.GS[:, ni], tmp.ap, self.gT[:, gi, :].unsqueeze(2).broadcast_to([128, 8, 2]),
                    ALU.mult, [tmp, self.gT], [self.GS])
        if self.debug:
            self.dma("sp", self.scr["dbg_mod"], self.MS.ap.rearrange("p a b -> p (a b)"), [self.MS], [B["dbg_mod"]])
        self.P.barrier()
        self.off = mark

    def wblk(self, j):
        c = j * 128
        for i in range(5):
            if self.wbounds[i] <= c < self.wbounds[i + 1]:
                return i
        raise AssertionError(j)

    def msc(self, vec, k, w):
        return self.MS[:, vec * 8 + k, w:w + 1]

    def phaseA(self):
        I, B, S = self.inp, self.dbufs, self.scr
        A = self.alloc
        self.qT = A("qT", [128, 4, NT], BF16)
        self.kT = A("kT", [128, NT], BF16)
        self.Vg = A("Vg", [128, 24, 2, 65], BF16)
        self.memset("pool", self.Vg.ap, 1.0, [self.Vg])
        self.markA = self.off
        W = self.W
        permT = A("permT", [128, 128], F32)
        self.dma("sp", permT.ap, I["permT"], [B["permT"]], [permT])
        xg = [A("xg%d" % i, [128, 8, 512], F32) for i in range(1)]
        hT = [A("hT%d" % i, [128, 8, 512], BF16) for i in range(2)]
        rope = A("rope", [128, 2, 512], F32)
        rstd = A("rstd", [128, 512], F32)
        sq = [A("sq%d" % i, [128, 512], BF16) for i in range(4)]
        tmpx = [A("tmpx%d" % i, [128, 512], F32) for i in range(2)]
        QF = [A("QF%d" % i, [128, 512], F32) for i in range(2)]
        t12 = [A("t12_%d" % i, [128, 512], F32) for i in range(4)]
        SGst = [A("SGst%d" % i, [128, 4, 512], BF16) for i in range(2)]
        ZRst = [A("ZRst%d" % i, [128, 4, 512], F32) for i in range(2)]
        KVo = [A("KVo%d" % i, [128, 4, 128], F32) for i in range(2)]
        xTv = I["xT"].rearrange("(k p) t -> p k t", p=128)
        ones, GS = self.onesb, self.GS

        def front(g):
            w = 0 if g < 4 else 1
            x = xg[0]
            h = hT[g % 2]
            self.dma("sp", x.ap, xTv[:, :, g * 512:(g + 1) * 512], [B["xT"]], [x])
            if g < 4:
                self.dma("sp", rope.ap, I["rope"][:, :, g * 512:(g + 1) * 512], [B["rope"]], [rope])
            bk = self.bank()
            for k in range(10):
                if k < 8:
                    self.act(sq[k % 4].ap, x[:, k, :], AF.Square, [x], [sq[k % 4]])
                if k >= 2:
                    j = k - 2
                    self.mm(bk, bk.ap, ones.ap, sq[j % 4].ap, [ones, sq[j % 4]], start=(j == 0), stop=(j == 7))
            self.powneg(rstd.ap, bk.ap, [bk], rstd, 0.5, scale=1.0 / D, bias=NORM_EPS)
            for k in range(8):
                t = tmpx[k % 2]
                self.tt("dve", t.ap, x[:, k, :], rstd.ap, ALU.mult, [x, rstd], [t])
                self.act(h[:, k, :], t.ap, AF.Identity, [t, GS, self.MS], [(h, k)],
                         scale=GS[:, 0, k, w:w + 1], bias=self.msc(0, k, w))

        def proj(g, h, j):
            bk = self.bank()
            for k in range(8):
                self.mm(bk, bk.ap, W[:, k, j * 128:(j + 1) * 128], h[:, k, :], [(W, self.wblk(j)), (h, k)],
                        start=(k == 0), stop=(k == 7))
            return bk

        def rope_a(bk):
            qf = QF[self._alt % 2]
            self._alt += 1
            self.copy("act", qf.ap, bk.ap, [bk], [qf])
            return qf

        def rope_b(qf, dst, dparts):
            b2 = self.bank()
            self.mm(b2, b2.ap, permT.ap, qf.ap, [permT, qf])
            i = 0 if qf is QF[0] else 1
            t1, t2 = t12[i], t12[2 + i]
            self.tt("dve", t1.ap, qf.ap, rope[:, 0, :], ALU.mult, [qf, rope], [t1])
            self.tt("dve", t2.ap, b2.ap, rope[:, 1, :], ALU.mult, [b2, rope], [t2])
            self.tt("pool", dst, t1.ap, t2.ap, ALU.add, [t1, t2], dparts)

        front(0)
        for g in range(6):
            sample = g < 4
            h = hT[g % 2]
            tok = slice(g * 512, (g + 1) * 512)
            pend = None
            for a in range(5):
                bk = proj(g, h, a)
                if a < 4:
                    dst, dparts = self.qT[:, a, tok], [(self.qT, (a, g))]
                else:
                    dst, dparts = self.kT[:, tok], [(self.kT, g)]
                if sample:
                    qf = rope_a(bk)
                    if pend is not None:
                        rope_b(*pend)
                    pend = (qf, dst, dparts)
                else:
                    self.copy("act", dst, bk.ap, [bk], dparts)
            if pend is not None:
                rope_b(*pend)
            for which, col in (("v", 5),) + ((("k", 4),) if not sample else ()):
                bk = self.bank()
                for ti in range(4):
                    for k in range(8):
                        self.mm(bk, bk[:, ti * 128:(ti + 1) * 128], h[:, k, ti * 128:(ti + 1) * 128],
                                W[:, k, col * 128:(col + 1) * 128], [(W, self.wblk(col)), (h, k)],
                                start=(k == 0), stop=(k == 7))
                if which == "v":
                    self.copy("dve", self.Vg[:, g * 4:(g + 1) * 4, :, 0:64],
                              bk.ap.rearrange("p (t g d) -> p t g d", t=4, g=2), [bk], [(self.Vg, g)])
                if not sample:
                    kv = KVo[0 if which == "v" else 1]
                    self.copy("act", kv.ap, bk.ap.rearrange("p (t d) -> p t d", t=4), [bk], [kv])
                    dst = self.out["newv" if which == "v" else "newk"]
                    r0 = (g - 4) * 512
                    self.dma("sp", dst[r0:r0 + 512, :].rearrange("(t p) d -> p t d", p=128), kv.ap,
                             [kv], [B["newv" if which == "v" else "newk"]])
            for q4 in range(4):
                st = SGst[q4 % 2]
                for jj in range(4):
                    bk = proj(g, h, 6 + q4 * 4 + jj)
                    self.act(st[:, jj, :], bk.ap, AF.Sigmoid, [bk], [(st, jj)])
                    if q4 == 1 and jj == 0 and g < 5:
                        front(g + 1)
                self.dma("sp", S["sgT"][q4 * 4:(q4 + 1) * 4, :, tok].rearrange("c p t -> p c t"), st.ap,
                         [st], [B["sgT"]])
            for q4, (c0, c1) in enumerate(((0, 4), (4, 8), (8, 12), (12, 15))):
                st = ZRst[q4 % 2]
                for c in range(c0, c1):
                    bk = proj(g, h, 22 + c)
                    self.copy(self.alt(), st[:, c - c0, :], bk.ap, [bk], [(st, c - c0)])
                self.dma("sp", S["zrT"][c0:c1, :, tok].rearrange("c p t -> p c t"), st[:, 0:c1 - c0, :],
                         [st], [B["zrT"]])
        if self.debug:
            self.dma("sp", S["dbg_q"], self.qT.ap, [self.qT], [B["dbg_q"]])
            self.dma("sp", S["dbg_k"], self.kT.ap, [self.kT], [B["dbg_k"]])
            self.dma("sp", S["dbg_v"], self.Vg.ap.rearrange("p t g d -> p (t g d)"), [self.Vg], [B["dbg_v"]])
        self.P.barrier()
        self.off = self.markA
        self.top = ARENA

    def phaseB1(self):
        I, B, S = self.inp, self.dbufs, self.scr
        A = self.alloc
        kcf = A("kcf", [128, 512], F32)
        self.dma("sp", kcf.ap, I["kcT"], [B["kcT"]], [kcf])
        kc = A("kc", [128, 512], BF16)
        self.copy("dve", kc.ap, kcf.ap, [kcf], [kc])
        vcf = A("vcf", [128, 4, 128], F32)
        self.dma("sp", vcf.ap, I["vc"], [B["vc"]], [vcf])
        Vc = A("Vc", [128, 4, 2, 65], BF16)
        self.memset("pool", Vc.ap, 1.0, [Vc])
        self.copy("dve", Vc[:, :, :, 0:64], vcf.ap.rearrange("p t (g d) -> p t g d", g=2), [vcf, Vc], [Vc])
        mprev = A("mprev", [128, 4, 128], BF16)
        mnext = A("mnext", [128, 4, 128], BF16)
        self.copy("dve", mprev.ap, self.masks[:, 2, :].unsqueeze(1).broadcast_to([128, 4, 128]), [self.masks], [mprev])
        self.copy("dve", mnext.ap, self.masks[:, 3, :].unsqueeze(1).broadcast_to([128, 4, 128]), [self.masks], [mnext])
        PT = [A("PT%d" % i, [128, 4, 128], BF16) for i in range(14)]
        osb = [A("osb%d" % i, [64, 512], F32) for i in range(2)]
        rden = [A("rden%d" % i, [128, 512], F32) for i in range(2)]
        ast = [A("ast%d" % i, [64, 8, 128], BF16) for i in range(2)]
        pti = [0]
        items = [(qt, g) for qt in range(24) for g in range(2)]

        def keys_of(qt):
            if qt < 16:
                keys = [("s", j, (mprev if j == qt - 1 else (mnext if j == qt + 1 else None)))
                        for j in (qt - 1, qt, qt + 1) if 0 <= j < 16]
                keys += [("c", j, None) for j in range(4)]
            else:
                s0 = 16 + ((qt - 16) // 2) * 2
                keys = [("s", s0, None), ("s", s0 + 1, None)]
            return keys

        def stage_s(qt, g):
            rows = slice(g * 64, (g + 1) * 64)
            pts = []
            for (kind, j, m) in keys_of(qt):
                bk = self.bank()
                if kind == "s":
                    lhsT, lrd = self.kT[rows, j * 128:(j + 1) * 128], self.kT
                else:
                    lhsT, lrd = kc[rows, j * 128:(j + 1) * 128], kc
                self.mm(bk, bk.ap.rearrange("p (a q) -> p a q", a=4), lhsT,
                        self.qT[rows, :, qt * 128:(qt + 1) * 128], [lrd, self.qT])
                pt = PT[pti[0] % 14]
                pti[0] += 1
                self.act(pt.ap, bk.ap.rearrange("p (a q) -> p a q", a=4), AF.Exp, [bk], [pt], scale=0.125)
                if m is not None:
                    self.tt("pool", pt.ap, pt.ap, m.ap, ALU.mult, [pt, m], [pt])
                pts.append((kind, j, pt))
            return pts

        def stage_o1(idx, qt, g, pts):
            bo = self.bank()
            for i, (kind, j, pt) in enumerate(pts):
                if kind == "s":
                    lhsT, lrd = self.Vg[:, j, g, :], self.Vg
                else:
                    lhsT, lrd = Vc[:, j, g, :], Vc
                self.mm(bo, bo[0:65, :], lhsT, pt.ap.rearrange("p a q -> p (a q)"), [lrd, pt],
                        start=(i == 0), stop=(i == len(pts) - 1))
            o = osb[idx % 2]
            rd = rden[idx % 2]
            self.copy("act", o.ap, bo[0:64, :], [bo], [o])
            self.tt("dve", rd[64:65, :].rearrange("p (a q) -> p a q", a=4),
                    bo[64:65, :].rearrange("p (a q) -> p a q", a=4),
                    self.esink[64:65, g * 4:(g + 1) * 4].unsqueeze(2).broadcast_to([1, 4, 128]),
                    ALU.add, [bo, self.esink], [rd])
            self.powneg(rd[64:65, :], rd[64:65, :], [rd], rd, 1.0)

        def stage_o2(idx, qt, g):
            st = ast[qt % 2]
            o = osb[idx % 2]
            rd = rden[idx % 2]
            bb = self.bank()
            self.mm(bb, bb[0:64, :], self.ones[64:65, 0:64], rd[64:65, :], [self.ones, rd])
            self.tt("dve", st[:, g * 4:(g + 1) * 4, :], o.ap.rearrange("p (a q) -> p a q", a=4),
                    bb[0:64, :].rearrange("p (a q) -> p a q", a=4), ALU.mult, [o, bb], [(st, g)])
            if g == 1:
                self.dma("sp", S["attnT"][:, :, qt * 128:(qt + 1) * 128], st.ap, [st], [B["attnT"]])

        nxt_pts = stage_s(*items[0])
        for idx, (qt, g) in enumerate(items):
            cur_pts = nxt_pts
            if idx + 1 < len(items):
                nxt_pts = stage_s(*items[idx + 1])
            stage_o1(idx, qt, g, cur_pts)
            if idx > 0:
                stage_o2(idx - 1, *items[idx - 1])
        stage_o2(len(items) - 1, *items[-1])
        self.P.barrier()
        self.off = self.markP

    def phaseB2(self):
        I, B, S = self.inp, self.dbufs, self.scr
        A = self.alloc

        def load(name, shape):
            t = A(name, shape, F32)
            self.dma("sp", t.ap, I[name], [B[name]], [t])
            return t

        mu = load("mu", [128, 15, 2])
        w0a0 = load("w0a0", [128, 2, 2, 4])
        w2 = load("w2", [128, 512])
        a2 = load("a2", [128, 512])
        g2 = load("g2", [128, 512])
        kvec = load("kvec", [128, 3, 4])
        lnT = load("lnT", [128, 2, 4])
        c0 = A("c0", [128, 15], F32)
        self.tt("dve", c0.ap, mu[:, :, 0], mu[:, :, 1], ALU.add, [mu], [c0])
        self.ts("dve", c0.ap, c0.ap, -1.0, 1.0, ALU.mult, ALU.add, [c0], [c0])
        omka = A("omka", [128, 4], F32)
        self.ts("dve", omka.ap, kvec[:, 1, :], -1.0, 1.0, ALU.mult, ALU.add, [kvec], [omka])
        nmask = A("nmask", [128, 2, 128], F32)
        self.ts("dve", nmask.ap, self.masks[:, 0:2, :], -1.0, None, ALU.mult, None, [self.masks], [nmask])
        selb = A("selb", [128, 2, 128], BF16)
        self.copy("dve", selb.ap, self.sel.ap, [self.sel], [selb])
        markB = self.off
        seqs = [(0, getattr(self, "b2_sample_t", 2048))] + [(2048 + 256 * i, 256) for i in range(4)]
        only = getattr(self, "b2_only", None)
        ident, identb, masks, sel, bones, ones = self.ident, self.identb, self.masks, self.sel, self.bones, self.ones

        def m4(t, i):
            return t[:, i, :].unsqueeze(1).broadcast_to([128, 4, 128])

        I4 = ident.ap.unsqueeze(1).broadcast_to([128, 4, 128])
        blocks = []
        for si, (s0, Tn) in enumerate(seqs):
            if only is not None and si not in only:
                continue
            for b in range(Tn // 256):
                blocks.append((si, s0, Tn, b))

        ZR = A("ZR", [128, 15, 258], F32)
        ZS = A("ZS", [128, 15, 256], F32)
        TW = A("TW", [128, 256], F32)
        SGz = A("SGz", [128, 256], F32)
        kk = A("kk", [128, 4, 256], F32)
        shp = [A("shp%d" % i, [128, 256], F32) for i in range(4)]
        r1 = {n: A("r_" + n, [128, 256], F32) for n in ("kq", "sq", "nrm", "bs", "bt", "rr")}
        rd = [{n: A("r%d_%s" % (d, n), [128, 256], F32)
               for n in ("tk", "KD", "BD", "CL", "EX", "Wm", "Wi", "Wx")} for d in range(2)]
        CT = A("CT", [128, 2, 2], F32)
        Aall = [A("Aall%d" % d, [128, 4, 256], F32) for d in range(2)]
        SGall = [A("SGall%d" % d, [128, 4, 256], F32) for d in range(2)]
        PB = []
        for i in range(2):
            pb = {}
            for n in ("RT", "KT", "BT", "AT"):
                pb[n] = [A("%s%d_%d" % (n, d, i), [128, 4, 256], BF16) for d in range(2)]
            pb["WC"] = [A("WC%d_%d" % (d, i), [128, 2, 4], F32) for d in range(2)]
            pb["VT"] = A("VT_%d" % i, [128, 2, 512], BF16)
            PB.append(pb)
        Gst = A("Gst", [128, 4, 256], F32)
        BVst = A("BVst", [128, 4, 256], F32)
        U12 = [[{n: A("u%d%d_%s" % (hg, d, n), [128, 4, 128], BF16) for n in ("PA", "PB", "PTA", "PTB", "QA", "QB")}
                for d in range(2)] for hg in range(2)]
        UU = []
        for hg_ in range(2):
            U_ = []
            for d in range(2):
                u = {n: A("u%d%d_%s" % (hg_, d, n), [128, 4, 128], BF16)
                     for n in ("AkT", "RHS", "X", "Ark", "Arb", "KBp", "BBp")}
                u["CG"] = A("u%d%d_CG" % (hg_, d), [128, 4, 128], BF16)
                u["CH"] = A("u%d%d_CH" % (hg_, d), [128, 4, 64], BF16)
                u["CZ"] = A("u%d%d_CZ" % (hg_, d), [128, 4, 64], F32)
                self.memset("pool", u["KBp"].ap, 0.0, [u["KBp"]])
                self.memset("pool", u["BBp"].ap, 0.0, [u["BBp"]])
                U_.append(u)
            UU.append(U_)
        Y0st = [A("Y0st%d" % i, [128, 512], F32) for i in range(2)]

        def prep_front(bi):
            si, s0, Tn, b = blocks[bi]
            nblk = Tn // 256
            pb = PB[bi % 2]
            t0 = s0 + 256 * b
            clo = 0 if b > 0 else 1
            chi = 258 if b < nblk - 1 else 257
            if clo == 1:
                self.memset("pool", ZR[:, :, 0:1], 0.0, [ZR])
            if chi == 257:
                self.memset("pool", ZR[:, :, 257:258], 0.0, [ZR])
            self.dma("sp", ZR[:, :, clo:chi],
                     S["zrT"][:, :, t0 - 1 + clo:t0 - 1 + chi].rearrange("c p t -> p c t"), [B["zrT"]], [ZR])
            for c in range(15):
                p0, p2 = shp[(c % 2) * 2], shp[(c % 2) * 2 + 1]
                self.act(p0.ap, ZR[:, c, 0:256], AF.Copy, [ZR, mu], [p0], scale=mu[:, c, 0:1])
                self.act(p2.ap, ZR[:, c, 2:258], AF.Copy, [ZR, mu], [p2], scale=mu[:, c, 1:2])
                self.ts("dve", ZS[:, c, :], ZR[:, c, 1:257], c0[:, c:c + 1], None, ALU.mult, None, [ZR, c0], [(ZS, c)])
                self.tt("pool", p0.ap, p0.ap, p2.ap, ALU.add, [p0, p2], [p0])
                self.tt("dve", ZS[:, c, :], ZS[:, c, :], p0.ap, ALU.add, [(ZS, c), p0], [(ZS, c)])
                yield
            self.act(TW.ap, ZS[:, 12, :], AF.Tanh, [(ZS, 12)], [TW])
            self.act(SGz.ap, ZS[:, 14, :], AF.Sigmoid, [(ZS, 14)], [SGz])
            yield
            for ci in range(2):
                bk = self.bank()
                for fc in range(4):
                    self.tr(bk, bk[:, fc * 128:(fc + 1) * 128], ZS[:, 8 + fc, ci * 128:(ci + 1) * 128], ident.ap,
                            [(ZS, 8 + fc), ident])
                self.copy("act" if ci == 0 else "dve", pb["VT"][:, ci, :], bk.ap, [bk], [(pb["VT"], ci)])
            yield
            for d in range(2):
                rows = slice(d * 64, (d + 1) * 64)
                for (wmat, src, dst, wi) in ((a2, ZS[rows, 13, :], Aall, 1), (w2, TW[rows, :], SGall, 0)):
                    yield
                    for f2 in range(2):
                        bk = self.bank()
                        for q in range(2):
                            fc = 2 * f2 + q
                            self.mm(bk, bk[:, q * 256:(q + 1) * 256], wmat[rows, fc * 128:(fc + 1) * 128], src,
                                    [wmat, (ZS, 13), TW])
                        for q in range(2):
                            fc = 2 * f2 + q
                            self.act(dst[d][:, fc, :], bk[:, q * 256:(q + 1) * 256], AF.Sigmoid, [bk, w0a0],
                                     [(dst[d], fc)], bias=w0a0[:, wi, d, fc:fc + 1])

        def prep_fc(bi, fc):
            si, s0, Tn, b = blocks[bi]
            pb = PB[bi % 2]
            t0 = s0 + 256 * b
            cgb = s0 // 128 + 2 * b
            RT, KT, BT, AT, WC = (pb[n] for n in ("RT", "KT", "BT", "AT", "WC"))
            bk = self.bank()
            self.mm(bk, bk[:, 0:256], g2[:, fc * 128:(fc + 1) * 128], SGz.ap, [g2, SGz])
            self.copy("act", Gst[:, fc, :], bk[:, 0:256], [bk], [(Gst, fc)])
            kq, sqt, nrm = r1["kq"], r1["sq"], r1["nrm"]
            self.act(kq.ap, ZS[:, 4 + fc, :], AF.Copy, [(ZS, 4 + fc), kvec], [kq], scale=kvec[:, 0, fc:fc + 1])
            self.act(sqt.ap, ZS[:, 4 + fc, :], AF.Square, [(ZS, 4 + fc), kvec], [sqt], scale=kvec[:, 0, fc:fc + 1])
            yield
            bk = self.bank()
            self.mm(bk, bk[:, 0:256], bones.ap, sqt.ap, [bones, sqt])
            self.powneg(nrm.ap, bk[:, 0:256], [bk], nrm, 0.5, bias=1e-30)
            self.tt("dve", kk[:, fc, :], kq.ap, nrm.ap, ALU.mult, [kq, nrm], [(kk, fc)])
            for d in range(2):
                yield
                R = rd[d]
                Ad = Aall[d][:, fc, :]
                sgd = SGall[d][:, fc, :]
                self.act(R["tk"].ap, Ad, AF.Identity, [(Aall[d], fc), kvec, omka], [R["tk"]],
                         scale=kvec[:, 1, fc:fc + 1], bias=omka[:, fc:fc + 1])
                self.tt("pool", R["KD"].ap, R["tk"].ap, ZS[:, 4 + fc, :], ALU.mult, [R["tk"], (ZS, 4 + fc)], [R["KD"]])
                self.tt("pool", R["BD"].ap, Ad, kk[:, fc, :], ALU.mult, [(Aall[d], fc), (kk, fc)], [R["BD"]])
                for ci in range(2):
                    cs = slice(ci * 128, (ci + 1) * 128)
                    self.P.add("dve", (lambda o, d1: (lambda e: e.tensor_tensor_scan(
                        out=o, data0=ones.ap, data1=d1, initial=0.0, op0=ALU.mult, op1=ALU.add)))(
                        R["CL"][:, cs], SGall[d][:, fc, cs]), reads=[ones, (SGall[d], fc)], writes=[R["CL"]])
                self.tt("pool", R["EX"].ap, R["CL"].ap, sgd, ALU.subtract, [R["CL"], (SGall[d], fc)], [R["EX"]])
                yield
                tot = R["CL"][:, 127:256:128]
                if d == 0:
                    self.act(R["Wm"].ap, R["CL"].ap, AF.Exp, [R["CL"]], [R["Wm"]], scale=CDEC)
                    self.act(R["Wi"].ap, R["CL"].ap, AF.Exp, [R["CL"]], [R["Wi"]], scale=-CDEC)
                    self.act(R["Wx"].ap, R["EX"].ap, AF.Exp, [R["EX"]], [R["Wx"]], scale=CDEC)
                    self.act(WC[0][:, :, fc], tot, AF.Exp, [R["CL"]], [(WC[0], fc)], scale=CDEC)
                else:
                    self.ts("dve", CT[:, 0, :], tot, CDEC, None, ALU.mult, None, [R["CL"]], [CT])
                    self.ts("dve", CT[:, 1, :], tot, -CDEC, None, ALU.mult, None, [R["CL"]], [CT])
                    for ci in range(2):
                        cs = slice(ci * 128, (ci + 1) * 128)
                        self.act(R["Wm"][:, cs], R["EX"][:, cs], AF.Exp, [R["EX"], CT], [R["Wm"]],
                                 scale=-CDEC, bias=CT[:, 0, ci:ci + 1])
                        self.act(R["Wi"][:, cs], R["EX"][:, cs], AF.Exp, [R["EX"], CT], [R["Wi"]],
                                 scale=CDEC, bias=CT[:, 1, ci:ci + 1])
                        self.act(R["Wx"][:, cs], R["CL"][:, cs], AF.Exp, [R["CL"], CT], [R["Wx"]],
                                 scale=-CDEC, bias=CT[:, 0, ci:ci + 1])
                    self.act(WC[1][:, :, fc], CT[:, 0, :], AF.Exp, [CT], [(WC[1], fc)])
                yield
                self.tt("dve", RT[d][:, fc, :], ZS[:, fc, :], R["Wm"].ap, ALU.mult, [(ZS, fc), R["Wm"]], [(RT[d], fc)])
                self.tt("pool", KT[d][:, fc, :], R["KD"].ap, R["Wi"].ap, ALU.mult, [R["KD"], R["Wi"]], [(KT[d], fc)])
                self.tt("dve", BT[d][:, fc, :], R["BD"].ap, R["Wi"].ap, ALU.mult, [R["BD"], R["Wi"]], [(BT[d], fc)])
                self.tt("pool", AT[d][:, fc, :], kk[:, fc, :], R["Wx"].ap, ALU.mult, [(kk, fc), R["Wx"]], [(AT[d], fc)])
            self.tt("pool", r1["bs"].ap, rd[0]["KD"].ap, rd[1]["KD"].ap, ALU.add, [rd[0]["KD"], rd[1]["KD"]], [r1["bs"]])
            self.act(r1["rr"].ap, ZS[:, fc, :], AF.Copy, [(ZS, fc), kvec], [r1["rr"]], scale=kvec[:, 2, fc:fc + 1])
            self.tt("dve", r1["bt"].ap, r1["bs"].ap, r1["rr"].ap, ALU.mult, [r1["bs"], r1["rr"]], [r1["bt"]])
            yield
            bk = self.bank()
            self.mm(bk, bk[:, 0:256], bones.ap, r1["bt"].ap, [bones, r1["bt"]])
            self.tt("dve", BVst[:, fc, :], bk[:, 0:256], ZS[:, 8 + fc, :], ALU.mult, [bk, (ZS, 8 + fc)], [(BVst, fc)])
            if fc == 3:
                tokb = slice(t0, t0 + 256)
                self.dma("sp", S["gTs"][:, :, tokb].rearrange("c p t -> p c t"), Gst.ap, [Gst], [B["gTs"]])
                self.dma("sp", S["bvT"][:, :, tokb].rearrange("c p t -> p c t"), BVst.ap, [BVst], [B["bvT"]])
                ncs = Tn // 128
                for d in range(2):
                    for ci in range(2):
                        cl = 2 * b + ci
                        cgx = s0 // 128 + (cl if d == 0 else ncs - 1 - cl)
                        for h2 in range(2):
                            self.dma("sp", S["wcD"][cgx, d * 64:(d + 1) * 64, h2, :],
                                     WC[d][h2 * 64:(h2 + 1) * 64, ci, :], [WC[d]], [B["wcD"]])

        def v4(bk):
            return bk.ap.rearrange("p (u m) -> p u m", u=4)

        def unit_quad(bi, ci):
            si, s0, Tn, b = blocks[bi]
            pb = PB[bi % 2]
            RT, KT, BT, AT, VT = (pb[n] for n in ("RT", "KT", "BT", "AT", "VT"))
            cg = s0 // 128 + 2 * b + ci
            cs = slice(ci * 128, (ci + 1) * 128)
            y0s = Y0st[ci]

            def hd(hg, u):
                h = 4 * hg + u
                return h, h // 2, slice((u % 2) * 64, (u % 2) * 64 + 64)
            mk = [((nmask, 0), (nmask, 1), (nmask, 1), (masks, 3)), ((nmask, 1), (nmask, 0), (nmask, 0), (masks, 2))]
            combos = [(hg, d) for hg in range(2) for d in range(2)]
            for (hg, d) in combos:
                u_ = U12[hg][d]
                for nm, XA_, XB_, mi in (("PA", AT, BT, 0), ("PTA", BT, AT, 1)):
                    bL = self.bank()

                    def f_(u, hard, bL=bL, d=d, hg=hg, XA_=XA_, XB_=XB_):
                        h, fc, R = hd(hg, u)
                        return self.mm(bL, bL[:, u * 128:(u + 1) * 128], XA_[d][R, fc, cs], XB_[d][R, fc, cs],
                                       [(AT[d], fc), (BT[d], fc)], hard=hard)
                    self.mm4(bL, None, f_)
                    mt, mi_ = mk[d][mi]
                    self.tt("dve", u_[nm].ap, v4(bL), m4(mt, mi_), ALU.mult, [bL, mt], [u_[nm]])
                self.tt("pool", u_["QA"].ap, u_["PTA"].ap, I4, ALU.add, [u_["PTA"], ident], [u_["QA"]])
            next(self._filler, None)
            cur = {c: ("PA", "PTA", "QA") for c in combos}
            nxt = {"PA": "PB", "PB": "PA", "PTA": "PTB", "PTB": "PTA", "QA": "QB", "QB": "QA"}
            def b3(hg, d):
                u_ = U12[hg][d]
                pn, ptn, qn = cur[(hg, d)]
                Pn_, Qc, Qn = u_[pn], u_[qn], u_[nxt[qn]]
                bq = self.bank()
                for u in range(4):
                    self.mm(bq, bq[:, u * 128:(u + 1) * 128], Pn_[:, u, :], Qc[:, u, :], [Pn_, Qc])
                self.tt("dve", Qn.ap, v4(bq), Qc.ap, ALU.add, [bq, Qc], [Qn])
                cur[(hg, d)] = (pn, ptn, nxt[qn])

            for step in range(6):
                for qi, (hg, d) in enumerate(combos):
                    if step > 0:
                        b3(hg, d)
                    u_ = U12[hg][d]
                    pn, ptn, qn = cur[(hg, d)]
                    Pc, PTc = u_[pn], u_[ptn]
                    Pn, PTn = u_[nxt[pn]], u_[nxt[ptn]]
                    b1 = self.bank()
                    for u in range(4):
                        self.mm(b1, b1[:, u * 128:(u + 1) * 128], PTc[:, u, :], Pc[:, u, :], [PTc, Pc])
                    self.copy("act", Pn.ap, v4(b1), [b1], [Pn])
                    if step < 5:
                        b2 = self.bank()
                        for u in range(4):
                            self.mm(b2, b2[:, u * 128:(u + 1) * 128], Pc[:, u, :], PTc[:, u, :], [PTc, Pc])
                        self.copy("act" if qi % 2 == 0 else "dve", PTn.ap, v4(b2), [b2], [PTn])
                    cur[(hg, d)] = (nxt[pn], nxt[ptn] if step < 5 else ptn, qn)
                    next(self._filler, None)
            for (hg, d) in combos:
                b3(hg, d)
            next(self._filler, None)
            def s_ak(hg, d):
                u_ = UU[hg][d]
                bk = self.bank()

                def f_(u, hard, bk=bk, d=d, hg=hg):
                    h, fc, R = hd(hg, u)
                    return self.mm(bk, bk[:, u * 128:(u + 1) * 128], KT[d][R, fc, cs], AT[d][R, fc, cs],
                                   [(AT[d], fc), (KT[d], fc)], hard=hard)
                self.mm4(bk, None, f_)
                mt, mi_ = mk[d][2]
                self.tt("dve", u_["AkT"].ap, v4(bk), m4(mt, mi_), ALU.mult, [bk, mt], [u_["AkT"]])

            def s_ar(hg, d):
                u_ = UU[hg][d]
                for nm, SRC in (("Ark", KT), ("Arb", BT)):
                    bk = self.bank()

                    def f_(u, hard, bk=bk, d=d, hg=hg, SRC=SRC):
                        h, fc, R = hd(hg, u)
                        return self.mm(bk, bk[:, u * 128:(u + 1) * 128], SRC[d][R, fc, cs], RT[d][R, fc, cs],
                                       [(SRC[d], fc), (RT[d], fc)], hard=hard)
                    self.mm4(bk, None, f_)
                    mt, mi_ = mk[d][3]
                    self.tt("dve", u_[nm].ap, v4(bk), m4(mt, mi_), ALU.mult, [bk, mt], [u_[nm]])

            def s_tra(hg, d):
                u_ = UU[hg][d]
                ac = slice(d * 64, (d + 1) * 64)
                btr = self.bank()
                btb = btr.ap.bitcast(BF16)
                for f2 in range(2):
                    fc = 2 * hg + f2
                    self.tr(btr, btb[:, f2 * 128:(f2 + 1) * 128], AT[d][:, fc, cs], identb.ap, [(AT[d], fc), identb])
                self.act(u_["RHS"][:, :, ac], btb[:, 0:256].rearrange("p (u i) -> p u i", u=4), AF.Copy,
                         [btr], [u_["RHS"]], scale=-1.0)

            def s_trkb(hg, d):
                u_ = UU[hg][d]
                bk = self.bank()
                bkb = bk.ap.bitcast(BF16)
                for f2 in range(2):
                    fc = 2 * hg + f2
                    self.tr(bk, bkb[:, f2 * 128:(f2 + 1) * 128], KT[d][:, fc, cs], identb.ap, [(KT[d], fc), identb])
                    self.tr(bk, bkb[:, 256 + f2 * 128:256 + (f2 + 1) * 128], BT[d][:, fc, cs], identb.ap,
                            [(BT[d], fc), identb])
                dsl = slice(d * 64, (d + 1) * 64)
                self.copy("act", u_["KBp"][:, :, dsl], bkb[:, 0:256].rearrange("p (u i) -> p u i", u=4),
                          [bk], [u_["KBp"]])
                self.copy("dve", u_["BBp"][:, :, dsl], bkb[:, 256:512].rearrange("p (u i) -> p u i", u=4),
                          [bk], [u_["BBp"]])

            def s_av(hg, d):
                u_ = UU[hg][d]
                avc = slice((1 - d) * 64, (2 - d) * 64)
                bav = self.bank()
                for u in range(4):
                    h, fc, R = hd(hg, u)
                    self.mm(bav, bav[:, u * 64:(u + 1) * 64], u_["AkT"][:, u, :], VT[:, ci, h * 64:(h + 1) * 64],
                            [u_["AkT"], (VT, ci)])
                self.copy("act", u_["RHS"][:, :, avc], bav[:, 0:256].rearrange("p (u i) -> p u i", u=4),
                          [bav], [u_["RHS"]])

            def s_x(hg, d):
                u_ = UU[hg][d]
                Qf = U12[hg][d][cur[(hg, d)][2]]
                bx = self.bank()
                for u in range(4):
                    self.mm(bx, bx[:, u * 128:(u + 1) * 128], Qf[:, u, :], u_["RHS"][:, u, :], [Qf, u_["RHS"]])
                self.copy("dve" if d == 0 else "act", u_["X"].ap, v4(bx), [bx], [u_["X"]])

            def s_y0(hg, d):
                u_ = UU[hg][d]
                avc = slice((1 - d) * 64, (2 - d) * 64)
                by = self.bank()
                for u in range(4):
                    h, fc, R = hd(hg, u)
                    self.mm(by, by[:, u * 64:(u + 1) * 64], u_["Ark"][:, u, :], VT[:, ci, h * 64:(h + 1) * 64],
                            [u_["Ark"], (VT, ci)], start=True, stop=False)
                    self.mm(by, by[:, u * 64:(u + 1) * 64], u_["Arb"][:, u, :], u_["X"][:, u, avc],
                            [u_["Arb"], u_["X"]], start=False, stop=True)
                ysl = y0s[:, hg * 256:(hg + 1) * 256]
                if d == 0:
                    self.copy("act", ysl, by[:, 0:256], [by], [(y0s, hg)])
                else:
                    self.tt("dve", ysl, by[:, 0:256], ysl, ALU.add, [by, (y0s, hg)], [(y0s, hg)])

            def s_g(hg, d):
                u_ = UU[hg][d]
                dsl = slice(d * 64, (d + 1) * 64)
                cgx = cg if d == 0 else (s0 // 128 + Tn // 128 - 1 - (2 * b + ci))
                bg = self.bank()
                for u in range(4):
                    h, fc, R = hd(hg, u)
                    self.mm(bg, bg[:, u * 128:(u + 1) * 128], u_["X"][:, u, :], u_["Arb"][:, u, :],
                            [u_["X"], u_["Arb"]], start=True, stop=False)
                    self.mm(bg, bg[:, u * 128:(u + 1) * 128], selb[R, d, :], RT[d][R, fc, cs],
                            [selb, (RT[d], fc)], start=False, stop=True)
                self.copy("act", u_["CG"][dsl], bg[dsl, :].rearrange("p (u m) -> p u m", u=4), [bg], [u_["CG"]])
                self.dma("sp", S["chG"][cgx, dsl, 4 * hg:4 * hg + 4, :], u_["CG"][dsl], [u_["CG"]], [B["chG"]])

            def s_hz(hg, d):
                u_ = UU[hg][d]
                dsl = slice(d * 64, (d + 1) * 64)
                avc = slice((1 - d) * 64, (2 - d) * 64)
                cgx = cg if d == 0 else (s0 // 128 + Tn // 128 - 1 - (2 * b + ci))
                bh = self.bank()
                for u in range(4):
                    h, fc, R = hd(hg, u)
                    self.mm(bh, bh[:, u * 64:(u + 1) * 64], u_["X"][:, u, :], u_["BBp"][:, u, dsl],
                            [u_["X"], u_["BBp"]], start=True, stop=False)
                    self.mm(bh, bh[:, u * 64:(u + 1) * 64], selb[R, d, :], identb[R, R],
                            [selb, identb], start=False, stop=True)
                for u in range(4):
                    h, fc, R = hd(hg, u)
                    self.mm(bh, bh[:, 256 + u * 64:256 + (u + 1) * 64], u_["KBp"][:, u, :],
                            VT[:, ci, h * 64:(h + 1) * 64], [u_["KBp"], (VT, ci)], start=True, stop=False)
                    self.mm(bh, bh[:, 256 + u * 64:256 + (u + 1) * 64], u_["BBp"][:, u, :], u_["X"][:, u, avc],
                            [u_["BBp"], u_["X"]], start=False, stop=True)
                self.copy("dve", u_["CH"][dsl], bh[dsl, 0:256].rearrange("p (u m) -> p u m", u=4), [bh], [u_["CH"]])
                self.copy("act", u_["CZ"][dsl], bh[dsl, 256:512].rearrange("p (u m) -> p u m", u=4), [bh], [u_["CZ"]])
                self.dma("sp", S["chH"][cgx, dsl, 4 * hg:4 * hg + 4, :], u_["CH"][dsl], [u_["CH"]], [B["chH"]])
                self.dma("sp", S["chZ"][cgx, dsl, 4 * hg:4 * hg + 4, :], u_["CZ"][dsl], [u_["CZ"]], [B["chZ"]])

            for stg in (s_ak, s_tra, s_ar, s_trkb, s_av, s_x, s_y0, s_g, s_hz):
                for hg in range(2):
                    for d in range(2):
                        stg(hg, d)
                    if stg in (s_ar, s_x, s_g):
                        next(self._filler, None)
            self.dma("sp", S["y0"][cg], y0s.ap, [y0s], [B["y0"]])

        nb = len(blocks)

        def prep_gen(bi):
            yield from prep_front(bi)
            for fc in range(4):
                yield
                yield from prep_fc(bi, fc)

        for _ in prep_gen(0):
            pass
        for bi in range(nb):
            self._filler = prep_gen(bi + 1) if bi + 1 < nb else iter(())
            for ci in range(2):
                unit_quad(bi, ci)
            for _ in self._filler:
                pass

        self.P.barrier()
        self.off = markB
        lo, hi = slice(0, 64), slice(64, 128)
        Yacc_all = A("Yacc", [128, 16, 512], F32)
        St = [A("St%d" % i, [128, 512], F32) for i in range(2)]
        G2 = [A("G2_%d" % i, [128, 8, 128], BF16) for i in range(2)]
        HBD = [A("HBD%d" % i, [128, 8, 128], BF16) for i in range(2)]
        Stb = [A("Stb%d" % i, [128, 512], BF16) for i in range(2)]
        Z2 = [A("Z2_%d" % i, [128, 512], F32) for i in range(2)]
        WCc = [A("WCc%d" % i, [128, 2, 4], F32) for i in range(2)]
        tsum = A("tsum", [128, 512], F32)
        for i in range(2):
            self.memset("pool", HBD[i].ap, 0.0, [HBD[i]])
        YaccP = A("YaccP", [128, 8, 512], F32)
        st32 = [A("st32_%d" % i, [128, 4, 32], F32) for i in range(2)]
        Dv = [A("Dv%d" % i, [128, 32, 64], F32) for i in range(2)]
        SQ = A("SQ", [128, 32, 64], F32)
        RW1 = [A("RW1_%d" % i, [128, 4, 512], F32) for i in range(2)]
        BVl = [A("BVl%d" % i, [128, 4, 512], F32) for i in range(2)]
        Gl = [A("Gl%d" % i, [128, 4, 512], F32) for i in range(2)]
        RWo = [A("RWo%d" % i, [128, 4, 512], BF16) for i in range(2)]
        post_groups = []
        for si, (s0, Tn) in enumerate(seqs):
            if only is not None and si not in only:
                continue
            NC = Tn // 128
            cg0 = s0 // 128
            if si == 0:
                Yacc = T(Yacc_all[:, 0:NC, :], "Yacc_v")
                Yacc.buf = Yacc_all.buf
                for c0_ in range(0, NC, 4):
                    post_groups.append((Yacc_all, c0_, min(4, NC - c0_), s0 + c0_ * 128))
            else:
                Yacc = T(YaccP[:, 2 * (si - 1):2 * si, :], "YaccP_v%d" % si)
                post_groups.append((Yacc, 0, 2, s0))
            self.dma("sp", Yacc.ap, S["y0"][cg0:cg0 + NC].rearrange("c p f -> p c f"), [B["y0"]], [Yacc])
            if si == 0:
                self.dma("sp", St[0].ap, I["st0"], [B["st0"]], [St[0]])
            else:
                self.memset("pool", St[0].ap, 0.0, [St[0]])
            self.copy("act", Stb[0].ap, St[0].ap, [St[0]], [Stb[0]])
            for k in range(NC):
                cf, cb = cg0 + k, cg0 + NC - 1 - k
                g2_, hb, z2, wcc = G2[k % 2], HBD[k % 2], Z2[k % 2], WCc[k % 2]
                sc_, sn = Stb[k % 2], St[(k + 1) % 2]
                self.dma("sp", g2_.ap, S["chG"][cf], [B["chG"]], [g2_])
                self.dma("sp", hb[lo, :, 0:64], S["chH"][cf, lo], [B["chH"]], [(hb, 0)])
                self.dma("sp", hb[hi, :, 64:128], S["chH"][cf, hi], [B["chH"]], [(hb, 1)])
                self.dma("sp", z2.ap.rearrange("p (h i) -> p h i", h=8), S["chZ"][cf], [B["chZ"]], [z2])
                self.dma("sp", wcc.ap, S["wcD"][cf], [B["wcD"]], [wcc])
                bs = self.bank()
                for h in range(8):
                    self.mm(bs, bs[:, h * 64:(h + 1) * 64], hb[:, h, :], sc_[:, h * 64:(h + 1) * 64], [hb, sc_])
                self.tt("dve", tsum.ap, bs.ap, z2.ap, ALU.add, [bs, z2], [tsum])
                self.tt("dve", sn.ap.rearrange("p (f q i) -> p f q i", f=4, q=2),
                        tsum.ap.rearrange("p (f q i) -> p f q i", f=4, q=2),
                        wcc.ap.rearrange("p q f -> p f q").unsqueeze(3).broadcast_to([128, 4, 2, 64]), ALU.mult,
                        [tsum, wcc], [sn])
                if k + 1 < NC:
                    self.copy("act", Stb[(k + 1) % 2].ap, sn.ap, [sn], [Stb[(k + 1) % 2]])
                for (half, cc) in ((lo, k), (hi, NC - 1 - k)):
                    by = self.bank()
                    for h in range(8):
                        self.mm(by, by[:, h * 64:(h + 1) * 64], g2_[half, h, :], sc_[half, h * 64:(h + 1) * 64], [g2_, sc_])
                    self.tt("pool" if False else "dve", Yacc[:, cc, :], by.ap, Yacc[:, cc, :], ALU.add,
                            [by, (Yacc, cc)], [(Yacc, cc)])
            if si > 0:
                self.dma("sp", self.out["newst"][si - 1], St[NC % 2].ap, [St[NC % 2]], [B["newst"]])
            if self.debug:
                self.dma("sp", S["dbg_y"][cg0:cg0 + NC].rearrange("c p f -> p c f"), Yacc.ap, [Yacc], [B["dbg_y"]])
        pg = [g for g in post_groups if g[0] is Yacc_all]
        pp = [g for g in post_groups if g[0] is not Yacc_all]
        if len(pp) == 4:
            pg += [(YaccP, 0, 4, 2048), (YaccP, 4, 4, 2048 + 512)]
        else:
            pg += pp
        for gi, (Ysrc, c0_, n, tok0) in enumerate(pg):
            nh = n * 8
            tokc = slice(tok0, tok0 + n * 128)
            Y = Ysrc[:, c0_:c0_ + n, :].rearrange("p c (h i) -> p (c h) i", h=8)
            dv, rw1, bvl, gl, rwo, s8 = Dv[gi % 2], RW1[gi % 2], BVl[gi % 2], Gl[gi % 2], RWo[gi % 2], st32[gi % 2]
            self.dma("sp", bvl[:, :, 0:n * 128], S["bvT"][:, :, tokc].rearrange("c p t -> p c t"), [B["bvT"]], [bvl])
            self.dma("sp", gl[:, :, 0:n * 128], S["gTs"][:, :, tokc].rearrange("c p t -> p c t"), [B["gTs"]], [gl])
            self.P.add("dve", (lambda o, i_: (lambda e: e.tensor_reduce(out=o, in_=i_, axis=AX.X, op=ALU.add)))(
                s8[:, 0, 0:nh], Y), reads=[Ysrc], writes=[s8])
            self.ts("dve", s8[:, 1, 0:nh], s8[:, 0, 0:nh], -1.0 / 64, None, ALU.mult, None, [s8], [s8])
            self.tt("dve", dv[:, 0:nh, :], Y, s8[:, 1, 0:nh].unsqueeze(2).broadcast_to([128, nh, 64]), ALU.add,
                    [Ysrc, s8], [dv])
            self.act(SQ[:, 0:nh, :], dv[:, 0:nh, :], AF.Square, [dv], [SQ])
            self.P.add("dve", (lambda o, i_: (lambda e: e.tensor_reduce(out=o, in_=i_, axis=AX.X, op=ALU.add)))(
                s8[:, 2, 0:nh], SQ[:, 0:nh, :]), reads=[SQ], writes=[s8])
            self.powneg(s8[:, 3, 0:nh], s8[:, 2, 0:nh], [s8], s8, 0.5, scale=1.0 / 64, bias=GN_EPS)
            self.tt("dve", dv[:, 0:nh, :], dv[:, 0:nh, :], s8[:, 3, 0:nh].unsqueeze(2).broadcast_to([128, nh, 64]),
                    ALU.mult, [dv, s8], [dv])
            for fc in range(4):
                bk = self.bank()
                for c in range(n):
                    self.tr(bk, bk[:, c * 128:(c + 1) * 128],
                            dv[:, c * 8 + 2 * fc:c * 8 + 2 * fc + 2, :].rearrange("p a b -> p (a b)"), ident.ap, [dv, ident])
                self.act(rw1[:, fc, 0:n * 128], bk[:, 0:n * 128], AF.Identity, [bk, lnT], [(rw1, fc)],
                         scale=lnT[:, 0, fc:fc + 1], bias=lnT[:, 1, fc:fc + 1])
            self.tt("dve", rw1[:, :, 0:n * 128], rw1[:, :, 0:n * 128], bvl[:, :, 0:n * 128], ALU.add, [rw1, bvl], [rw1])
            self.tt("pool", rwo[:, :, 0:n * 128], rw1[:, :, 0:n * 128], gl[:, :, 0:n * 128], ALU.mult, [rw1, gl], [rwo])
            self.dma("sp", S["rwT"][:, :, tokc].rearrange("c p t -> p c t"), rwo[:, :, 0:n * 128], [rwo], [B["rwT"]])
        self.P.barrier()
        self.off = self.markP

    def phaseC1(self):
        I, B, S = self.inp, self.dbufs, self.scr
        A = self.alloc
        wpa = A("wpa", [64, 8, D], BF16)
        self.dma("pool", wpa.ap, I["w_pa"], [B["w_pa"]], [wpa])
        wpb = A("wpb", [128, 4, D], BF16)
        self.dma("pool", wpb.ap, I["w_pb"], [B["w_pb"]], [wpb])
        wo = A("wo", [128, 8, D], BF16)
        self.dma("pool", wo.ap, I["w_out"], [B["w_out"]], [wo])
        at = [A("at%d" % i, [64, 8, 512], BF16) for i in range(2)]
        rw = [A("rw%d" % i, [128, 4, 512], BF16) for i in range(2)]
        sg = [A("sg%d" % i, [128, 16, 512], BF16) for i in range(2)]
        xg = [A("xc%d" % i, [128, 8, 512], F32) for i in range(2)]
        mg = A("mg", [128, 8, 512], BF16)
        x1 = A("x1", [128, 8, 512], F32)
        t1 = [A("c1t%d" % i, [128, 512], F32) for i in range(2)]
        t2 = [A("c1u%d" % i, [128, 512], F32) for i in range(2)]
        xTv = I["xT"].rearrange("(k p) t -> p k t", p=128)

        def loads(g):
            tok = slice(g * 512, (g + 1) * 512)
            self.dma("sp", at[g % 2].ap, S["attnT"][:, :, tok], [B["attnT"]], [at[g % 2]])
            self.dma("sp", rw[g % 2].ap, S["rwT"][:, :, tok].rearrange("c p t -> p c t"), [B["rwT"]], [rw[g % 2]])
            self.dma("sp", sg[g % 2].ap, S["sgT"][:, :, tok].rearrange("c p t -> p c t"), [B["sgT"]], [sg[g % 2]])
            self.dma("sp", xg[g % 2].ap, xTv[:, :, tok], [B["xT"]], [xg[g % 2]])

        loads(0)
        for g in range(6):
            w = 0 if g < 4 else 1
            tok = slice(g * 512, (g + 1) * 512)
            if g < 5:
                loads(g + 1)
            a_, r_, s_, x_ = at[g % 2], rw[g % 2], sg[g % 2], xg[g % 2]
            for n in range(8):
                ns = slice(n * 128, (n + 1) * 128)
                ba = self.bank()
                for h in range(8):
                    self.mm(ba, ba.ap, wpa[:, h, ns], a_[:, h, :], [wpa, a_], start=(h == 0), stop=(h == 7))
                bb = self.bank()
                for fc in range(4):
                    self.mm(bb, bb.ap, wpb[:, fc, ns], r_[:, fc, :], [wpb, r_], start=(fc == 0), stop=(fc == 3))
                u1, u2 = t1[n % 2], t2[n % 2]
                self.tt("dve", u1.ap, ba.ap, s_[:, n, :], ALU.mult, [ba, s_], [u1])
                self.tt("dve", u2.ap, bb.ap, s_[:, 8 + n, :], ALU.mult, [bb, s_], [u2])
                self.tt("pool", mg[:, n, :], u1.ap, u2.ap, ALU.add, [u1, u2], [(mg, n)])
            for n in range(8):
                ns = slice(n * 128, (n + 1) * 128)
                bo = self.bank()
                for k in range(8):
                    self.mm(bo, bo.ap, wo[:, k, ns], mg[:, k, :], [wo, (mg, k)], start=(k == 0), stop=(k == 7))
                self.stt("dve", x1[:, n, :], bo.ap, self.msc(2, n, w), x_[:, n, :], ALU.mult, ALU.add,
                         [bo, self.MS, x_], [(x1, n)])
            self.dma("sp", S["x1T"][:, :, tok].rearrange("c p t -> p c t"), x1.ap, [x1], [B["x1T"]])
        self.P.barrier()
        self.off = self.markP

    def phaseC2(self):
        I, B, S = self.inp, self.dbufs, self.scr
        A = self.alloc
        wup = A("wup", [128, 8, 2 * DFF], BF16)
        fbounds = [0, 4, 10, 16, NFC]
        for bi_ in range(4):
            for half_ in range(2):
                c0_ = half_ * DFF + fbounds[bi_] * 128
                c1_ = half_ * DFF + fbounds[bi_ + 1] * 128
                self.dma("pool", wup[:, :, c0_:c1_], I["w_up"][:, :, c0_:c1_], [B["w_up"]], [(wup, bi_)])

        def fblk(fc_):
            return max(i for i in range(4) if fbounds[i] <= fc_)
        wdn = A("wdn", [128, NFC, D], BF16)
        for q in range(2):
            self.dma("pool", wdn[:, q * 11:(q + 1) * 11, :], I["w_dn"][:, q * 11:(q + 1) * 11, :], [B["w_dn"]], [(wdn, q)])
        cw = A("cw", [128, 2 * NFC, 3], F32)
        self.dma("sp", cw.ap, I["convw"], [B["convw"]], [cw])
        cb = A("cb", [128, 2 * NFC], F32)
        self.dma("sp", cb.ap, I["convb"], [B["convb"]], [cb])
        X1 = [A("X1_%d" % i, [128, 8, 258], F32) for i in range(2)]
        h2 = A("h2", [128, 8, 258], BF16)
        aT = A("aT", [128, NFC, 256], BF16)
        x2 = A("x2", [128, 8, 256], F32)
        yst = A("yst", [128, 8, 256], F32)
        rstd = A("rstd2", [128, 258], F32)
        sq = [A("sq2_%d" % i, [128, 258], BF16) for i in range(2)]
        tx = [A("tx2_%d" % i, [128, 258], F32) for i in range(2)]
        cv = [A("cv%d" % i, [128, 256], F32) for i in range(2)]
        cg_ = [A("cg%d" % i, [128, 256], F32) for i in range(2)]
        sgl = [A("sgl%d" % i, [128, 256], F32) for i in range(2)]
        ones, GS = self.onesb, self.GS
        seqs = [(0, 2048)] + [(2048 + 256 * i, 256) for i in range(4)]
        blocks = []
        for (s0, Tn) in seqs:
            for b in range(Tn // 256):
                blocks.append((s0, Tn, b))

        def load(bi):
            s0, Tn, b = blocks[bi]
            t0 = s0 + 256 * b
            clo = 0 if b > 0 else 1
            chi = 258 if b < Tn // 256 - 1 else 257
            X = X1[bi % 2]
            self.dma("sp", X[:, :, clo:chi], S["x1T"][:, :, t0 - 1 + clo:t0 - 1 + chi].rearrange("c p t -> p c t"),
                     [B["x1T"]], [X])

        h2s = [h2, A("h2b", [128, 8, 258], BF16)]

        def geom(bi):
            s0, Tn, b = blocks[bi]
            clo = 0 if b > 0 else 1
            chi = 258 if b < Tn // 256 - 1 else 257
            return s0, Tn, b, (0 if s0 == 0 else 1), s0 + 256 * b, clo, chi

        def front(bi):
            s0, Tn, b, w, t0, clo, chi = geom(bi)
            wd = chi - clo
            cs = slice(clo, chi)
            X = X1[bi % 2]
            hh = h2s[bi % 2]
            bk = self.bank()
            for k in range(8):
                s_ = sq[k % 2]
                self.act(s_[:, cs], X[:, k, cs], AF.Square, [X], [s_])
                self.mm(bk, bk[:, 0:wd], ones.ap, s_[:, cs], [ones, s_], start=(k == 0), stop=(k == 7))
            self.powneg(rstd[:, cs], bk[:, 0:wd], [bk], rstd, 0.5, scale=1.0 / D, bias=NORM_EPS)
            for k in range(8):
                t_ = tx[k % 2]
                self.tt("dve", t_[:, cs], X[:, k, cs], rstd[:, cs], ALU.mult, [X, rstd], [t_])
                self.act(hh[:, k, cs], t_[:, cs], AF.Identity, [t_, GS, self.MS], [(hh, k)],
                         scale=GS[:, 1, k, w:w + 1], bias=self.msc(3, k, w))

        def upconv(bi):
            s0, Tn, b, w, t0, clo, chi = geom(bi)
            wd = chi - clo
            cs = slice(clo, chi)
            hh = h2s[bi % 2]
            o = 1 - clo
            for fc in range(NFC):
                res = []
                for half in range(2):
                    col = half * DFF + fc * 128
                    ch = half * NFC + fc
                    bu = self.bank()
                    for k in range(8):
                        self.mm(bu, bu[:, 0:wd], wup[:, k, col:col + 128], hh[:, k, cs], [(wup, fblk(fc)), (hh, k)],
                                start=(k == 0), stop=(k == 7))
                    dst = (cv if half == 0 else cg_)[fc % 2]
                    self.act(dst.ap, bu[:, o:o + 256], AF.Identity, [bu, cw, cb], [dst],
                             scale=cw[:, ch, 1:2], bias=cb[:, ch:ch + 1])
                    j0 = 1 if clo == 1 else 0
                    self.stt("dve", dst[:, j0:256], bu[:, o + j0 - 1:o + 255], cw[:, ch, 0:1], dst[:, j0:256],
                             ALU.mult, ALU.add, [bu, cw, dst], [dst])
                    j1 = 255 if chi == 257 else 256
                    self.stt("dve", dst[:, 0:j1], bu[:, o + 1:o + 1 + j1], cw[:, ch, 2:3], dst[:, 0:j1],
                             ALU.mult, ALU.add, [bu, cw, dst], [dst])
                    res.append(dst)
                sl = sgl[fc % 2]
                self.act(sl.ap, res[1].ap, AF.Silu, [res[1]], [sl])
                self.tt("pool", aT[:, fc, :], sl.ap, res[0].ap, ALU.mult, [sl, res[0]], [(aT, fc)])

        def downfinal(bi):
            s0, Tn, b, w, t0, clo, chi = geom(bi)
            X = X1[bi % 2]
            for n in range(8):
                ns = slice(n * 128, (n + 1) * 128)
                bd = self.bank()
                for fc in range(NFC):
                    self.mm(bd, bd[:, 0:256], wdn[:, fc, ns], aT[:, fc, :], [(wdn, fc // 11), (aT, fc)],
                            start=(fc == 0), stop=(fc == NFC - 1))
                self.stt("dve", x2[:, n, :], bd[:, 0:256], self.msc(5, n, w), X[:, n, 1:257], ALU.mult, ALU.add,
                         [bd, self.MS, X], [(x2, n)])
            bk = self.bank()
            for n in range(8):
                s_ = sq[n % 2]
                self.act(s_[:, 0:256], x2[:, n, :], AF.Square, [(x2, n)], [s_])
                self.mm(bk, bk[:, 0:256], ones.ap, s_[:, 0:256], [ones, s_], start=(n == 0), stop=(n == 7))
            self.powneg(rstdf.ap, bk[:, 0:256], [bk], rstdf, 0.5, scale=1.0 / D, bias=NORM_EPS)
            for n in range(8):
                self.stt("dve", yst[:, n, :], x2[:, n, :], self.gT[:, 2, n:n + 1],
                         rstdf.ap, ALU.mult, ALU.mult, [(x2, n), self.gT, rstdf], [(yst, n)])
            self.dma("sp", self.out["yT"].rearrange("(k p) t -> p k t", p=128)[:, :, t0:t0 + 256], yst.ap,
                     [yst], [B["yT"]])

        rstdf = A("rstdf", [128, 256], F32)
        load(0)
        load(1)
        front(0)
        for bi in range(len(blocks)):
            upconv(bi)
            if bi + 1 < len(blocks):
                front(bi + 1)
            downfinal(bi)
            if bi + 2 < len(blocks):
                load(bi + 2)
        self.P.barrier()


def _fm(v, nchunk):
    return np.ascontiguousarray(np.asarray(v, np.float32).reshape(nchunk, 128).T)


def _consts():
    idx = np.arange(128)
    p = idx[:, None]
    f = idx[None, :]
    masks = np.stack([(f < p), (f > p), (f <= p), (f >= p)], axis=1).astype(np.float32)
    ident = np.eye(128, dtype=np.float32)
    blockones = np.kron(np.eye(2, dtype=np.float32), np.ones((64, 64), np.float32))
    sel = np.zeros((128, 2, 128), np.float32)
    for dd in range(2):
        sel[idx, dd, dd * 64 + idx % 64] = 1.0
    T = NS
    row = np.repeat(np.arange(T // 64), 64).astype(np.float32)
    col = np.tile(np.arange(64), T // 64).astype(np.float32)
    freqs = (np.float32(10000.0) ** (-np.arange(16, dtype=np.float32) / np.float32(16))).astype(np.float32)
    rope = np.zeros((128, 2, T), np.float32)
    permT = np.zeros((128, 128), np.float32)
    for pp in range(128):
        dd = pp % 64
        pos = row if dd < 32 else col
        ang = (pos * freqs[dd % 16]).astype(np.float32)
        first = (dd % 32) < 16
        rope[pp, 0] = np.cos(ang)
        rope[pp, 1] = (-np.sin(ang)) if first else np.sin(ang)
        partner = pp + 16 if first else pp - 16
        permT[partner, pp] = 1.0
    return dict(ident=ident, masks=masks, blockones=blockones, sel=sel, rope=rope, permT=permT)


def _prep_shared(inp):
    f = lambda a: np.ascontiguousarray(np.asarray(a, np.float32))
    sh = {}
    sh["w_ada"] = f(inp["w_ada"][0])
    sh["b_adaT"] = _fm(inp["b_ada"][0], 48)
    sh["gT"] = np.ascontiguousarray(np.stack([_fm(inp["g_norm1"][0], 8), _fm(inp["g_norm2"][0], 8),
                                              _fm(inp["g_final"], 8)], axis=1))
    w_in = f(inp["w_in"][0])
    qcols = []
    for a in range(4):
        qcols += list(range(a * 64, a * 64 + 64)) + list(range((4 + a) * 64, (4 + a) * 64 + 64))
    cols = qcols + list(range(512, WCOLS))
    sh["w_in"] = np.ascontiguousarray(w_in[:, cols])
    sh["sink"] = f(inp["attn_sink"][0]).reshape(1, 8)
    sh["w_pa"] = np.ascontiguousarray(f(inp["w_proj_a"][0]).reshape(8, 64, D).transpose(1, 0, 2))
    sh["w_pb"] = np.ascontiguousarray(f(inp["w_proj_b"][0]).reshape(4, 128, D).transpose(1, 0, 2))
    sh["w_out"] = np.ascontiguousarray(f(inp["w_out"][0]).reshape(8, 128, D).transpose(1, 0, 2))
    sh["w_up"] = np.ascontiguousarray(f(inp["w_ffn_up"][0]).reshape(8, 128, 2 * DFF).transpose(1, 0, 2))
    sh["w_dn"] = np.ascontiguousarray(f(inp["w_ffn_down"][0]).reshape(NFC, 128, D).transpose(1, 0, 2))
    cw = f(inp["ffn_conv_w"][0])
    sh["convw"] = np.ascontiguousarray(cw.T.reshape(2 * NFC, 128, 3).transpose(1, 0, 2))
    sh["convb"] = _fm(inp["ffn_conv_b"][0], 2 * NFC)
    mu = f(inp["rwkv_mu"][0])
    sh["mu"] = np.ascontiguousarray(mu.T.reshape(15, 128, 2).transpose(1, 0, 2))
    w0 = f(inp["rwkv_w0"][0])
    a0 = f(inp["rwkv_a0"][0])
    w0a0 = np.stack([w0, a0], axis=0).reshape(2, 2, 4, 128)
    sh["w0a0"] = np.ascontiguousarray(w0a0.transpose(3, 0, 1, 2))
    sh["w2"] = f(inp["rwkv_w2"][0]).reshape(128, 512)
    sh["a2"] = f(inp["rwkv_a2"][0]).reshape(128, 512)
    sh["g2"] = f(inp["rwkv_g2"][0])
    sh["kvec"] = np.ascontiguousarray(np.stack([_fm(inp["rwkv_k_k"][0], 4), _fm(inp["rwkv_k_a"][0], 4),
                                                _fm(inp["rwkv_r_k"][0].reshape(-1), 4)], axis=1))
    sh["lnT"] = np.ascontiguousarray(np.stack([_fm(inp["rwkv_ln_g"][0], 4), _fm(inp["rwkv_ln_b"][0], 4)], axis=1))
    sh.update(_consts())
    return sh


def _prep_core(inp, c):
    f = lambda a: np.ascontiguousarray(np.asarray(a, np.float32))
    m = {}
    xs = f(inp["x_sample"][c])
    xp = f(inp["x_prompt"][4 * c:4 * c + 4]).reshape(1024, D)
    m["xT"] = np.ascontiguousarray(np.concatenate([xs, xp], axis=0).T)
    cv = np.stack([_fm(inp["c"][c], 8), _fm(inp["c_ctx"], 8)], axis=2)
    m["cvec"] = np.ascontiguousarray(cv)
    ck = f(inp["cache_k"][c, 0]).reshape(512, 128)
    m["kcT"] = np.ascontiguousarray(ck.T)
    cvv = f(inp["cache_v"][c, 0]).reshape(4, 128, 128)
    m["vc"] = np.ascontiguousarray(cvv.transpose(1, 0, 2))
    st = f(inp["state_rwkv"][c, 0])
    m["st0"] = np.ascontiguousarray(st.transpose(0, 3, 1, 2).reshape(128, 512))
    return m


_CACHE = {}


def _get_nc(debug=False, upto="C2"):
    key = (debug, upto)
    if key not in _CACHE:
        _CACHE[key] = KB(debug=debug, upto=upto)
        _CACHE[key].build()
    return _CACHE[key]


def kernel(**inputs):
    kb = _get_nc()
    sh = _prep_shared(inputs)
    in_maps = []
    for c in range(NCORES):
        m = dict(sh)
        m.update(_prep_core(inputs, c))
        in_maps.append(m)
    res = run_bass_kernel_spmd(kb.nc, in_maps, core_ids=list(range(NCORES)))
    y_prompt = np.zeros((32, 256, D), np.float32)
    y_sample = np.zeros((8, 2048, D), np.float32)
    new_k = np.zeros((32, 1, 256, 2, 64), np.float32)
    new_v = np.zeros((32, 1, 256, 2, 64), np.float32)
    new_s = np.zeros((32, 1, 2, 8, 64, 64), np.float32)
    for c in range(NCORES):
        r = res.results[c]
        y = np.asarray(r["yT"]).T
        y_sample[c] = y[:2048]
        y_prompt[4 * c:4 * c + 4] = y[2048:].reshape(4, 256, D)
        new_k[4 * c:4 * c + 4, 0] = np.asarray(r["newk"]).reshape(4, 256, 2, 64)
        new_v[4 * c:4 * c + 4, 0] = np.asarray(r["newv"]).reshape(4, 256, 2, 64)
        st = np.asarray(r["newst"]).reshape(4, 2, 64, 8, 64)
        new_s[4 * c:4 * c + 4, 0] = st.transpose(0, 1, 3, 4, 2)
    return (y_prompt, y_sample, new_k, new_v, new_s)
```
